# Optimizing a Trainium2 kernel written in Bass

```python
import jax, jax.numpy as jnp
from jax import lax
import numpy as np

D_MODEL = 1024
BATCH = 8
SEQ = 2048
DEPTH = 2

GRID_W = 64
CTX_LEN = 256

HEAD_DIM = 64
ATTN_HEADS = (D_MODEL // 2) // HEAD_DIM
ATTN_KV_HEADS = 2
ATTN_WIDTH = ATTN_HEADS * HEAD_DIM
KV_WIDTH = ATTN_KV_HEADS * HEAD_DIM
Q_BLOCK = 128
ROPE_THETA = 10000.0

HG_DIM = 64
HG_HEADS = (D_MODEL // 4) // HG_DIM
HG_WIDTH = HG_HEADS * HG_DIM
HG_CHUNK = 64
F_MIN = 1e-30

FT_DIM = 64
FT_GROUPS = (D_MODEL // 4) // FT_DIM
FT_WIDTH = FT_GROUPS * FT_DIM

MIX_WIDTH = ATTN_WIDTH + HG_WIDTH + FT_WIDTH
PROJ_WIDTH = ATTN_WIDTH + 2 * KV_WIDTH + 5 * HG_WIDTH + FT_WIDTH

D_FF = 2816
CONV_W = 3

ALPHA = (2 * DEPTH) ** 0.25
BETA = (8 * DEPTH) ** -0.25
EPS = 1e-6

kernel_name = 'hymba_style_hgrn2_fnet_gqa_prefix_dit'


def split_proj(p):
    sizes = [ATTN_WIDTH, KV_WIDTH, KV_WIDTH, HG_WIDTH, HG_WIDTH, HG_WIDTH, HG_WIDTH, HG_WIDTH, FT_WIDTH]
    return jnp.split(p, [int(s) for s in np.cumsum(sizes)[:-1]], axis=-1)


def rms_norm(x, g):
    xf = x.astype(jnp.float32)
    y = xf * lax.rsqrt(jnp.mean(xf * xf, axis=-1, keepdims=True) + EPS)
    return (y * g.astype(jnp.float32)).astype(x.dtype)


def layer_norm(x, g, b):
    xf = x.astype(jnp.float32)
    mu = jnp.mean(xf, axis=-1, keepdims=True)
    var = jnp.mean(jnp.square(xf - mu), axis=-1, keepdims=True)
    y = (xf - mu) * lax.rsqrt(var + EPS)
    return (y * g.astype(jnp.float32) + b.astype(jnp.float32)).astype(x.dtype)


def to_heads(a, n):
    b, t, _ = a.shape
    return a.reshape(b, t, n, -1).transpose(0, 2, 1, 3)


def axial_rope(x, row, col):
    half = x.shape[-1] // 2
    nf = half // 2
    inv = ROPE_THETA ** (-jnp.arange(nf, dtype=jnp.float32) / nf)

    def rotate(xa, pos):
        ang = pos.astype(jnp.float32)[:, None] * inv
        cos = jnp.cos(ang).astype(x.dtype)
        sin = jnp.sin(ang).astype(x.dtype)
        x1, x2 = xa[..., :nf], xa[..., nf:]
        return jnp.concatenate([x1 * cos - x2 * sin, x1 * sin + x2 * cos], axis=-1)

    return jnp.concatenate([rotate(x[..., :half], row), rotate(x[..., half:], col)], axis=-1)


def group_queries(q):
    b, nh, t, hd = q.shape
    return q.reshape(b, ATTN_KV_HEADS, nh // ATTN_KV_HEADS, t, hd)


def attend(q, k, v):
    s = jnp.einsum('bhgqd,bhkd->bhgqk', q, k).astype(jnp.float32) * (HEAD_DIM ** -0.5)
    p = jax.nn.softmax(s, axis=-1).astype(v.dtype)
    return jnp.einsum('bhgqk,bhkd->bhgqd', p, v)


def latent_attention(q, k_all, v_all):
    b, kvh, g, t, hd = q.shape
    nb = t // Q_BLOCK
    qb = jnp.moveaxis(q.reshape(b, kvh, g, nb, Q_BLOCK, hd), 3, 0)
    o = lax.map(lambda qblk: attend(qblk, k_all, v_all), qb)
    return o.transpose(1, 0, 4, 2, 3, 5).reshape(b, t, kvh * g * hd)


def gla_scan(q, k, v, logf, s0, reverse):
    if reverse:
        q, k, v, logf = (jnp.flip(a, axis=2) for a in (q, k, v, logf))
    b, h, t, dk = q.shape
    n = t // HG_CHUNK

    def chunks(a):
        return a.reshape(b, h, n, HG_CHUNK, a.shape[-1]).transpose(2, 0, 1, 3, 4)

    tri = jnp.tril(jnp.ones((HG_CHUNK, HG_CHUNK), dtype=bool))[:, :, None]

    def step(state, inp):
        qc, kc, vc, lf = inp
        cum = jnp.cumsum(lf, axis=2)
        diff = cum[:, :, :, None, :] - cum[:, :, None, :, :]
        decay = jnp.where(tri, jnp.exp(jnp.where(tri, diff, 0.0)), 0.0)
        scores = jnp.einsum('bhtk,bhtsk,bhsk->bhts', qc, decay, kc)
        o = (jnp.einsum('bhts,bhsv->bhtv', scores, vc)
             + jnp.einsum('bhtk,bhkv->bhtv', qc * jnp.exp(cum), state))
        last = cum[:, :, -1, :]
        state = (jnp.exp(last)[..., None] * state
                 + jnp.einsum('bhsk,bhsv->bhkv', kc * jnp.exp(last[:, :, None, :] - cum), vc))
        return state, o

    s_final, o = lax.scan(step, s0, (chunks(q), chunks(k), chunks(v), chunks(logf)))
    o = o.transpose(1, 2, 0, 3, 4).reshape(b, h, t, v.shape[-1])
    if reverse:
        o = jnp.flip(o, axis=2)
    return o, s_final


def hgrn_qv(hq, hi):
    q = to_heads(jax.nn.silu(hq), HG_HEADS).astype(jnp.float32)
    v = to_heads(hi, HG_HEADS).astype(jnp.float32)
    return q, v


def hgrn_forget(hf, lb):
    lbh = lb.reshape(HG_HEADS, 1, HG_DIM)
    z = to_heads(hf, HG_HEADS).astype(jnp.float32)
    f = lbh + (1.0 - lbh) * jax.nn.sigmoid(z)
    logf = jnp.log(jnp.maximum(f, F_MIN))
    return logf, 1.0 - f


def hgrn_out(o, hg, g):
    b, h, t, dv = o.shape
    on = rms_norm(o.transpose(0, 2, 1, 3), g)
    gate = jax.nn.silu(hg.astype(jnp.float32)).reshape(b, t, h, dv)
    return (on * gate).reshape(b, t, h * dv).astype(hg.dtype)


def fourier_mix(u, w):
    b, t, _ = u.shape
    ug = u.reshape(b, t, FT_GROUPS, FT_DIM).astype(jnp.float32)
    z = jnp.fft.fft2(ug, axes=(1, 3), norm='ortho').real.astype(u.dtype)
    return jnp.einsum('btgc,gcd->btgd', z, w).reshape(b, t, FT_WIDTH)


def conv_ffn(h, w_up, conv_w, conv_b, w_down):
    u = h @ w_up
    up = jnp.pad(u, ((0, 0), (1, 1), (0, 0)))
    u = up[:, :-2] * conv_w[0] + up[:, 1:-1] * conv_w[1] + up[:, 2:] * conv_w[2] + conv_b
    a, g = jnp.split(u, 2, axis=-1)
    return (a * jax.nn.silu(g)) @ w_down


def setup_inputs(seed: int = 0) -> dict:
    key = jax.random.key(seed)
    ks = jax.random.split(key, 21)
    f32 = jnp.float32

    def nrm(k, shape, s):
        return jax.random.normal(k, shape, f32) * s

    return {
        'x': nrm(ks[0], (BATCH, SEQ, D_MODEL), 1.0),
        'c': nrm(ks[1], (BATCH, D_MODEL), 1.0),
        'ctx': nrm(ks[2], (BATCH, CTX_LEN, D_MODEL), 1.0),
        'c_ctx': nrm(ks[3], (D_MODEL,), 1.0),
        'w_ada': nrm(ks[4], (DEPTH, D_MODEL, 6 * D_MODEL), 0.5 * D_MODEL ** -0.5),
        'b_ada': nrm(ks[5], (DEPTH, 6 * D_MODEL), 0.02),
        'w_in': nrm(ks[6], (DEPTH, D_MODEL, PROJ_WIDTH), D_MODEL ** -0.5),
        'q_norm': 1.0 + nrm(ks[7], (DEPTH, HEAD_DIM), 0.02),
        'k_norm': 1.0 + nrm(ks[8], (DEPTH, HEAD_DIM), 0.02),
        'hg_lb': nrm(ks[9], (DEPTH, 2, HG_WIDTH), 0.5),
        'hg_norm': 1.0 + nrm(ks[10], (DEPTH, HG_DIM), 0.02),
        'ft_w': nrm(ks[11], (DEPTH, FT_GROUPS, FT_DIM, FT_DIM), FT_DIM ** -0.5),
        'w_out': nrm(ks[12], (DEPTH, MIX_WIDTH, D_MODEL), BETA * MIX_WIDTH ** -0.5),
        'ln1_g': 1.0 + nrm(ks[13], (DEPTH, D_MODEL), 0.02),
        'ln1_b': nrm(ks[14], (DEPTH, D_MODEL), 0.02),
        'w_up': nrm(ks[15], (DEPTH, D_MODEL, 2 * D_FF), D_MODEL ** -0.5),
        'conv_w': nrm(ks[16], (DEPTH, CONV_W, 2 * D_FF), CONV_W ** -0.5),
        'conv_b': nrm(ks[17], (DEPTH, 2 * D_FF), 0.02),
        'w_down': nrm(ks[18], (DEPTH, D_FF, D_MODEL), BETA * D_FF ** -0.5),
        'ln2_g': 1.0 + nrm(ks[19], (DEPTH, D_MODEL), 0.02),
        'ln2_b': nrm(ks[20], (DEPTH, D_MODEL), 0.02),
    }


def reference(x, c, ctx, c_ctx, w_ada, b_ada, w_in, q_norm, k_norm, hg_lb, hg_norm, ft_w, w_out,
              ln1_g, ln1_b, w_up, conv_w, conv_b, w_down, ln2_g, ln2_b):
    b, t, _ = x.shape
    rows = t // GRID_W
    row = jnp.repeat(jnp.arange(rows, dtype=jnp.int32), GRID_W)
    col = jnp.tile(jnp.arange(GRID_W, dtype=jnp.int32), rows)

    lb_soft = jax.nn.softmax(hg_lb.astype(jnp.float32), axis=0)
    lower = jnp.cumsum(lb_soft, axis=0) - lb_soft[:1]
    zero_state = jnp.zeros((b, HG_HEADS, HG_DIM, HG_DIM), jnp.float32)

    for l in range(DEPTH):
        need_ctx = l < DEPTH - 1

        mod = (jax.nn.silu(c) @ w_ada[l] + b_ada[l])[:, None, :]
        sh1, sc1, g1, sh2, sc2, g2 = jnp.split(mod, 6, axis=-1)
        mod_c = jax.nn.silu(c_ctx) @ w_ada[l] + b_ada[l]
        sh1c, sc1c, g1c, sh2c, sc2c, g2c = jnp.split(mod_c, 6, axis=-1)

        h = x * (1 + sc1) + sh1
        hc = ctx * (1 + sc1c) + sh1c
        q, k, v, hq, hi, hff, hfb, hg, ft = split_proj(h @ w_in[l])
        qc, kc, vc, hqc, hic, hffc, hfbc, hgc, ftc = split_proj(hc @ w_in[l])

        kc_h = rms_norm(to_heads(kc, ATTN_KV_HEADS), k_norm[l])
        vc_h = to_heads(vc, ATTN_KV_HEADS)
        q_h = axial_rope(rms_norm(to_heads(q, ATTN_HEADS), q_norm[l]), row, col)
        k_h = axial_rope(rms_norm(to_heads(k, ATTN_KV_HEADS), k_norm[l]), row, col)
        k_all = jnp.concatenate([kc_h, k_h], axis=2)
        v_all = jnp.concatenate([vc_h, to_heads(v, ATTN_KV_HEADS)], axis=2)
        attn = latent_attention(group_queries(q_h), k_all, v_all)

        lb_f, lb_b = lower[l, 0], lower[l, 1]
        qa_c, va_c = hgrn_qv(hqc, hic)
        lfc_f, kfc_f = hgrn_forget(hffc, lb_f)
        lfc_b, kfc_b = hgrn_forget(hfbc, lb_b)
        oc_f, s_f = gla_scan(qa_c, kfc_f, va_c, lfc_f, zero_state, False)
        oc_b, s_b = gla_scan(qa_c, kfc_b, va_c, lfc_b, zero_state, True)
        qa, va = hgrn_qv(hq, hi)
        lf_f, kf_f = hgrn_forget(hff, lb_f)
        lf_b, kf_b = hgrn_forget(hfb, lb_b)
        o_f, _ = gla_scan(qa, kf_f, va, lf_f, s_f, False)
        o_b, _ = gla_scan(qa, kf_b, va, lf_b, s_b, True)
        rec = hgrn_out(o_f + o_b, hg, hg_norm[l])

        four = fourier_mix(ft, ft_w[l])

        mix = jnp.concatenate([attn, rec, four], axis=-1) @ w_out[l]
        x1 = layer_norm(ALPHA * x + g1 * mix, ln1_g[l], ln1_b[l])
        ffn = conv_ffn(x1 * (1 + sc2) + sh2, w_up[l], conv_w[l], conv_b[l], w_down[l])
        x = layer_norm(ALPHA * x1 + g2 * ffn, ln2_g[l], ln2_b[l])

        if need_ctx:
            qc_h = group_queries(rms_norm(to_heads(qc, ATTN_HEADS), q_norm[l]))
            attn_c = attend(qc_h, kc_h, vc_h).transpose(0, 3, 1, 2, 4).reshape(b, ctx.shape[1], ATTN_WIDTH)
            rec_c = hgrn_out(oc_f + oc_b, hgc, hg_norm[l])
            four_c = fourier_mix(ftc, ft_w[l])
            mix_c = jnp.concatenate([attn_c, rec_c, four_c], axis=-1) @ w_out[l]
            ctx1 = layer_norm(ALPHA * ctx + g1c * mix_c, ln1_g[l], ln1_b[l])
            ffn_c = conv_ffn(ctx1 * (1 + sc2c) + sh2c, w_up[l], conv_w[l], conv_b[l], w_down[l])
            ctx = layer_norm(ALPHA * ctx1 + g2c * ffn_c, ln2_g[l], ln2_b[l])

    return x
```

```python
import numpy as np
import concourse.bass as bass
import concourse.mybir as mybir
from concourse.bass_utils import run_bass_kernel_spmd

F32 = mybir.dt.float32
BF16 = mybir.dt.bfloat16
I32 = mybir.dt.int32
AF = mybir.ActivationFunctionType
ALU = mybir.AluOpType
AX = mybir.AxisListType

_ESZ = {F32: 4, BF16: 2, I32: 4}


def _acc(ap):
    t = ap.tensor
    name = t.name
    esz = _ESZ[ap.dtype]
    pat = ap.ap
    off = int(ap.offset)
    sp = str(ap.space)
    if "DRAM" in sp.upper() or "HBM" in sp.upper():
        lo = off
        hi = off
        for st, cnt in pat:
            if st >= 0:
                hi += st * (cnt - 1)
            else:
                lo += st * (cnt - 1)
        return ("d:" + name, 0, 1, lo * esz, (hi + 1) * esz)
    row = pat[0][0]
    p0 = off // row
    col = off % row
    p1 = p0 + pat[0][1]
    lo = col
    hi = col
    for st, cnt in pat[1:]:
        if st >= 0:
            hi += st * (cnt - 1)
        else:
            lo += st * (cnt - 1)
    if "PSUM" in sp.upper():
        b0 = (lo * esz) // 2048
        b1 = ((hi + 1) * esz + 2047) // 2048
        return ("p:" + name, 0, 128, b0 * 2048, b1 * 2048)
    key = "s:" + name
    return (key, p0, p1, lo * esz, (hi + 1) * esz)


class Prog:
    ENGS = ("pe", "act", "dve", "pool", "sp")

    def __init__(self, nc, n_dma_sems=20):
        self.nc = nc
        self.ops = []
        self.deps = []
        self.recs = {}
        self.n_dma_sems = n_dma_sems

    def op(self, eng, fn, reads=(), writes=(), dma=False):
        oid = len(self.ops)
        deps = set()
        for ap in reads:
            k, p0, p1, lo, hi = _acc(ap)
            lst = self.recs.setdefault(k, [])
            for r in lst:
                if r[4] == "w" and r[0] < p1 and p0 < r[1] and r[2] < hi and lo < r[3]:
                    deps.add(r[5])
        wr = []
        for ap in writes:
            k, p0, p1, lo, hi = _acc(ap)
            wr.append((k, p0, p1, lo, hi))
            lst = self.recs.setdefault(k, [])
            keep = []
            for r in lst:
                if r[0] < p1 and p0 < r[1] and r[2] < hi and lo < r[3]:
                    deps.add(r[5])
                    if p0 <= r[0] and r[1] <= p1 and lo <= r[2] and r[3] <= hi:
                        continue
                keep.append(r)
            self.recs[k] = keep
        for ap in reads:
            k, p0, p1, lo, hi = _acc(ap)
            lst = self.recs[k]
            if not dma:
                for i, r in enumerate(lst):
                    if r[4] == "r" and r[6] == eng and r[0] == p0 and r[1] == p1 and r[2] == lo and r[3] == hi:
                        lst[i] = [p0, p1, lo, hi, "r", oid, eng]
                        break
                else:
                    lst.append([p0, p1, lo, hi, "r", oid, eng])
            else:
                lst.append([p0, p1, lo, hi, "r", oid, eng])
        for (k, p0, p1, lo, hi) in wr:
            self.recs[k].append([p0, p1, lo, hi, "w", oid, eng])
        deps.discard(oid)
        self.ops.append((eng, fn, dma))
        self.deps.append(deps)
        return oid

    def emit(self):
        nc = self.nc
        ops, deps = self.ops, self.deps
        n = len(ops)
        needed = [False] * n
        for i in range(n):
            ei = ops[i][0]
            for d in deps[i]:
                ed, _, ddma = ops[d]
                if ed == "pe" and ei == "pe":
                    continue
                needed[d] = True
        sig = [None] * n
        cnt = {e: 0 for e in self.ENGS}
        dma_i = 0
        dma_prev = {}
        dma_use = [0] * self.n_dma_sems
        for i in range(n):
            e, _, isd = ops[i]
            if isd:
                if e == "sp":
                    s = dma_prev.get("sp", 0) % 14
                    dma_prev["sp"] = dma_prev.get("sp", 0) + 1
                else:
                    s = 14 + dma_prev.get("pool", 0) % (self.n_dma_sems - 14)
                    dma_prev["pool"] = dma_prev.get("pool", 0) + 1
                dma_use[s] += 1
                sig[i] = ("dma%d" % s, 16 * dma_use[s])
            elif needed[i]:
                cnt[e] += 1
                sig[i] = (e, cnt[e])
        streams = {e: [] for e in self.ENGS}
        waited = {e: {} for e in self.ENGS}
        for i in range(n):
            e, fn, isd = ops[i]
            w = {}
            for d in deps[i]:
                ed = ops[d][0]
                if ed == "pe" and e == "pe" and not ops[d][2]:
                    continue
                s, v = sig[d]
                if v > w.get(s, 0):
                    w[s] = v
            if isd:
                s, v = sig[i]
                if v > 16:
                    w[s] = max(w.get(s, 0), v - 16)
            wl = []
            for s, v in w.items():
                if waited[e].get(s, 0) < v:
                    waited[e][s] = v
                    wl.append((s, v))
            streams[e].append((wl, fn, sig[i], isd))
        self.stats = {e: len(streams[e]) for e in self.ENGS}
        self.stats["signals"] = dict(cnt)
        sem_names = list(self.ENGS) + ["dma%d" % s for s in range(self.n_dma_sems)]
        import contextlib
        with contextlib.ExitStack() as st:
            sems = {nm: st.enter_context(nc.semaphore("sem_" + nm)) for nm in sem_names}
            block = st.enter_context(nc.Block())

            def run(engobj, ename):
                for wl, fn, sg, isd in streams[ename]:
                    for s, v in wl:
                        engobj.wait_ge(sems[s], v)
                    ins = fn(engobj)
                    if sg is not None:
                        ins.then_inc(sems[sg[0]], 16 if isd else 1)
                if ename == "sp":
                    for s in range(self.n_dma_sems):
                        if dma_use[s] > 0:
                            engobj.wait_ge(sems["dma%d" % s], 16 * dma_use[s])

            @block.sync
            def _(e):
                run(e, "sp")

            @block.scalar
            def _(e):
                run(e, "act")

            @block.vector
            def _(e):
                run(e, "dve")

            @block.gpsimd
            def _(e):
                run(e, "pool")

            @block.tensor
            def _(e):
                run(e, "pe")


class Arena:
    def __init__(self, tensor, nwords):
        self.t = tensor
        self.n = nwords
        self.off = 0
        self.marks = []

    def f32(self, cols, parts=128):
        a = self.t[0:parts, self.off:self.off + cols]
        self.off += cols
        assert self.off <= self.n, ("arena overflow", self.off, self.n)
        return a

    def bf16(self, cols, parts=128):
        w = (cols + 1) // 2
        a = self.t[0:parts, self.off:self.off + w].bitcast(BF16)
        self.off += w
        assert self.off <= self.n, ("arena overflow", self.off, self.n)
        return a

    def push(self):
        self.marks.append(self.off)
        self.peak = max(getattr(self, 'peak', 0), self.off)

    def pop(self):
        self.peak = max(getattr(self, 'peak', 0), self.off)
        self.off = self.marks.pop()


D = 1024
T = 2048
CT = 256
NT = CT + T
DEPTH = 2
DFF = 2816
NJ = DFF // 128
ALPHA = (2 * DEPTH) ** 0.25
EPS = 1e-6
GRID_W = 64
ROPE_THETA = 10000.0

WIN_SLABS = ["q0", "q1", "q2", "q3", "ka", "kb", "v", "hq0", "hq1", "hi0", "hi1",
             "hff0", "hff1", "hfb0", "hfb1", "hg0", "hg1", "ft0", "ft1"]
WIN_COL = {"q0": 0, "q1": 128, "q2": 256, "q3": 384, "ka": 512, "v": 640, "hq0": 768, "hq1": 896,
           "hi0": 1024, "hi1": 1152, "hff0": 1280, "hff1": 1408, "hfb0": 1536, "hfb1": 1664,
           "hg0": 1792, "hg1": 1920, "ft0": 2048, "ft1": 2176}
SLAB_IDX = {n: i for i, n in enumerate(WIN_SLABS)}
NSLAB = len(WIN_SLABS)

_V = {}
_off = 0
for _l in range(DEPTH):
    for _n, _w in (("bada", 48), ("qg", 1), ("kg", 1), ("hgn", 1), ("ln1g", 8), ("ln1b", 8),
                   ("ln2g", 8), ("ln2b", 8), ("convw", 132), ("convb", 44), ("ftw", 128)):
        _V[(_n, _l)] = (_off, _w)
        _off += _w
_V["hglb"] = (_off, 8)
_off += 8
_V["cc"] = (_off, 16)
_off += 16
NV = _off

_C = {}
_off = 0
for _n, _w in (("ident", 128), ("blk64", 128), ("ones1024", 128), ("ones", 128), ("rperm", 128),
               ("c64", 128), ("s64", 128), ("scanmask", 512), ("hmask", 2)):
    _C[_n] = (_off, _w)
    _off += _w
NC = _off

CH_ALL = [(0, 256, 1), (256, 512, 0), (768, 512, 0), (1280, 512, 0), (1792, 512, 0)]
CH_LAT = CH_ALL[1:]
FFN_LAT = [(256 + i * 410, 410) for i in range(4)] + [(256 + 1640, 408)]
FFN_CTX = [(0, 256)]


def host_consts():
    c = np.zeros((128, NC), np.float32)
    o = _C["ident"][0]
    c[:, o:o + 128] = np.eye(128, dtype=np.float32)
    o = _C["blk64"][0]
    for h in range(2):
        c[h * 64:(h + 1) * 64, o + h * 64:o + (h + 1) * 64] = 1.0 / 64
    o = _C["ones1024"][0]
    c[:, o:o + 128] = 1.0 / 1024
    o = _C["ones"][0]
    c[:, o:o + 128] = 1.0
    o = _C["rperm"][0]
    sign = np.zeros(128, np.float32)
    for p in range(128):
        w = (p % 64) % 32
        partner = p + 16 if w < 16 else p - 16
        c[partner, o + p] = 1.0
        sign[p] = -1.0 if w < 16 else 1.0
    k = np.arange(64)
    ang = 2 * np.pi * np.outer(k, k) / 64.0
    o = _C["c64"][0]
    o2 = _C["s64"][0]
    for h in range(2):
        c[h * 64:(h + 1) * 64, o + h * 64:o + (h + 1) * 64] = np.cos(ang)
        c[h * 64:(h + 1) * 64, o2 + h * 64:o2 + (h + 1) * 64] = np.sin(ang)
    o = _C["hmask"][0]
    c[0:64, o] = 1.0
    c[64:128, o + 1] = 1.0
    o = _C["scanmask"][0]
    c[:, o:o + 512] = 1.0
    c[:, o:o + 512:64] = 0.0
    s = np.arange(128)[:, None]
    t = np.arange(128)[None, :]
    same = (s // 64) == (t // 64)
    msk = np.concatenate([(same & (s <= t)), (same & (s >= t))], axis=1).astype(np.int32)
    nf = 16
    inv = (ROPE_THETA ** (-np.arange(nf, dtype=np.float32) / nf)).astype(np.float32)
    pos_row = (np.arange(T) // GRID_W).astype(np.float32)
    pos_col = (np.arange(T) % GRID_W).astype(np.float32)
    cosT = np.zeros((128, T), np.float32)
    sinT = np.zeros((128, T), np.float32)
    for p in range(128):
        i = p % 64
        pos = pos_row if i < 32 else pos_col
        a = (pos * inv[i % 16]).astype(np.float32)
        cosT[p] = np.cos(a)
        sinT[p] = np.sin(a) * sign[p]
    rope = np.concatenate([cosT, sinT], axis=1).astype(np.float32)
    import ml_dtypes
    rope = rope.astype(ml_dtypes.bfloat16)
    import ml_dtypes

    def dft_tabs(n):
        k = np.arange(n, dtype=np.float64)
        a = 2 * np.pi * np.outer(k, k) / n
        sc = 1.0 / np.sqrt(n * 64.0)
        return (np.cos(a) * sc), (-np.sin(a) * sc)

    Cl, Sl = dft_tabs(T)

    def lay(m, n):
        nt = n // 128
        nch = n // 256
        r = m.reshape(nt, 128, nch, 256).transpose(2, 1, 0, 3)
        return np.ascontiguousarray(r).astype(ml_dtypes.bfloat16)

    dftl = np.stack([lay(Cl, T), lay(Sl, T)], axis=1)
    Cc, Sc = dft_tabs(CT)
    dftc = np.stack([lay(Cc, CT), lay(Sc, CT)], axis=1)
    return c, msk, rope, dftl, dftc


def host_layout(inp):
    f = lambda a: np.ascontiguousarray(np.asarray(a, dtype=np.float32))
    x, c, ctx, c_ctx = f(inp["x"]), f(inp["c"]), f(inp["ctx"]), f(inp["c_ctx"])
    w_ada, b_ada, w_in = f(inp["w_ada"]), f(inp["b_ada"]), f(inp["w_in"])
    B = x.shape[0]
    sh = {}
    sh["wada"] = np.ascontiguousarray(
        w_ada.reshape(DEPTH, 8, 128, 48, 128).transpose(0, 3, 2, 1, 4)).reshape(DEPTH * 48 * 128, 1024)
    colidx = []
    for nme in WIN_SLABS:
        if nme == "kb":
            colidx.append(np.concatenate([np.arange(576, 640), np.arange(512, 576)]))
        else:
            colidx.append(WIN_COL[nme] + np.arange(128))
    colidx = np.concatenate(colidx)
    win = w_in[:, :, colidx].reshape(DEPTH, 8, 128, NSLAB, 128).transpose(0, 3, 2, 1, 4)
    sh["win"] = np.ascontiguousarray(win).reshape(DEPTH * NSLAB * 128, 1024)
    sh["wout"] = np.ascontiguousarray(
        f(inp["w_out"]).reshape(DEPTH, 8, 128, 1024).transpose(0, 2, 1, 3)).reshape(DEPTH * 128, 8192)
    wup = f(inp["w_up"]).reshape(DEPTH, 8, 128, 2, NJ, 128).transpose(0, 4, 2, 1, 3, 5)
    sh["wup"] = np.ascontiguousarray(wup).reshape(DEPTH * NJ * 128, 2048)
    sh["wdown"] = np.ascontiguousarray(
        f(inp["w_down"]).reshape(DEPTH, NJ, 128, 8, 128).transpose(0, 2, 3, 1, 4)).reshape(DEPTH * 128, NJ * 1024)
    vec = np.zeros((128, NV), np.float32)

    def put(key, arr):
        o, w = _V[key]
        vec[:, o:o + w] = arr.reshape(128, w)

    for l in range(DEPTH):
        put(("bada", l), b_ada[l].reshape(48, 128).T)
        put(("qg", l), np.tile(f(inp["q_norm"])[l], 2)[:, None])
        put(("kg", l), np.tile(f(inp["k_norm"])[l], 2)[:, None])
        put(("hgn", l), np.tile(f(inp["hg_norm"])[l], 2)[:, None])
        for nme, key in (("ln1g", "ln1_g"), ("ln1b", "ln1_b"), ("ln2g", "ln2_g"), ("ln2b", "ln2_b")):
            put((nme, l), f(inp[key])[l].reshape(8, 128).T)
        put(("convw", l), f(inp["conv_w"])[l].reshape(3, 44, 128).transpose(2, 1, 0))
        put(("convb", l), f(inp["conv_b"])[l].reshape(44, 128).T)
        put(("ftw", l), f(inp["ft_w"])[l].reshape(2, 128, 64).transpose(1, 0, 2))
    put("hglb", f(inp["hg_lb"]).reshape(DEPTH, 2, 2, 128).transpose(3, 0, 1, 2))
    cst, msk, rope, dftl, dftc = host_consts()
    sh["cst"] = cst
    sh["msk"] = msk
    sh["rope"] = rope
    sh["dftl"] = dftl.reshape(8 * 2 * 128, 16 * 256)
    sh["dftc"] = dftc.reshape(2 * 128, 2 * 256)
    maps = []
    for b in range(B):
        m = dict(sh)
        xa = np.concatenate([ctx[b], x[b]], axis=0)
        m["xT0"] = np.ascontiguousarray(xa.T.reshape(8, 128, NT).transpose(1, 0, 2)).reshape(128, 8 * NT)
        v = vec.copy()
        o, w = _V["cc"]
        cc = np.stack([c[b], c_ctx], axis=1)
        v[:, o:o + w] = cc.reshape(8, 128, 2).transpose(1, 0, 2).reshape(128, 16)
        m["vecs"] = v
        maps.append(m)
    return maps


ARENA_WORDS = 53000


def build_program(n_layers=DEPTH, dbg=False):
    import contextlib
    nc = bass.Bass("TRN2", target_bir_lowering=False)
    dram = lambda n, s, d=F32, k="ExternalInput": nc.dram_tensor(n, list(s), d, kind=k).ap()
    xT0 = dram("xT0", [128, 8 * NT])
    vecs_d = dram("vecs", [128, NV])
    cst_d = dram("cst", [128, NC])
    msk_d = dram("msk", [128, 256], I32)
    rope_d = dram("rope", [128, 2 * T], BF16)
    dftl_d = dram("dftl", [8 * 2 * 128, 16 * 256], BF16)
    dftc_d = dram("dftc", [2 * 128, 2 * 256], BF16)
    wada_d = dram("wada", [DEPTH * 48 * 128, 1024])
    win_d = dram("win", [DEPTH * NSLAB * 128, 1024])
    wout_d = dram("wout", [DEPTH * 128, 8192])
    wup_d = dram("wup", [DEPTH * NJ * 128, 2048])
    wdown_d = dram("wdown", [DEPTH * 128, NJ * 1024])
    yT = dram("yT", [128, 8 * T], F32, "ExternalOutput")
    winb = dram("winb", [DEPTH * NSLAB * 128, 1024], BF16, "Internal")
    woutb = dram("woutb", [DEPTH * 128, 8192], BF16, "Internal")
    wupb = dram("wupb", [DEPTH * NJ * 128, 2048], BF16, "Internal")
    wdownb = dram("wdownb", [DEPTH * 128, NJ * 1024], BF16, "Internal")
    dbg_t = {}
    if dbg:
        for l in range(n_layers):
            dbg_t["mix%d" % l] = dram("d_mix%d" % l, [128, 8 * NT], BF16, "ExternalOutput")
            dbg_t["x1_%d" % l] = dram("d_x1_%d" % l, [128, 8 * NT], F32, "ExternalOutput")
            dbg_t["x_%d" % l] = dram("d_x_%d" % l, [128, 8 * NT], F32, "ExternalOutput")
        dbg_t["mod"] = dram("d_mod", [128, DEPTH * 96], F32, "ExternalOutput")

    st = contextlib.ExitStack()
    with st:
        art = st.enter_context(nc.sbuf_tensor("arena", [128, ARENA_WORDS], F32))
        pst = st.enter_context(nc.psum_tensor("ps", [128, 4096], F32))
        A = Arena(art, ARENA_WORDS)
        P = Prog(nc)

        def bank(b, n=512, off=0):
            return pst[:, b * 512 + off:b * 512 + off + n]

        def mm(out, lhsT, rhs, start=True, stop=True):
            P.op("pe", lambda e: e.matmul(out, lhsT=lhsT, rhs=rhs, start=start, stop=stop), [lhsT, rhs], [out])

        def tr(out, in_, ident):
            P.op("pe", lambda e: e.transpose(out, in_, ident), [in_, ident], [out])

        def act(out, in_, func, scale=1.0, bias=0.0):
            rd = [in_] + [a for a in (scale, bias) if not isinstance(a, (int, float))]
            P.op("act", lambda e: e.activation(out, in_, func, bias=bias, scale=scale), rd, [out])

        def tt(out, in0, in1, op, eng="dve"):
            P.op(eng, lambda e: e.tensor_tensor(out=out, in0=in0, in1=in1, op=op), [in0, in1], [out])

        def ts(out, in0, s1, s2, op0, op1=None, eng="dve"):
            rd = [in0] + [a for a in (s1, s2) if a is not None and not isinstance(a, (int, float))]
            if op1 is None:
                P.op(eng, lambda e: e.tensor_scalar(out=out, in0=in0, scalar1=s1, scalar2=None, op0=op0), rd, [out])
            else:
                P.op(eng, lambda e: e.tensor_scalar(out=out, in0=in0, scalar1=s1, scalar2=s2, op0=op0, op1=op1), rd, [out])

        def stt(out, in0, scalar, in1, op0, op1):
            rd = [in0, in1] + ([] if isinstance(scalar, (int, float)) else [scalar])
            P.op("dve", lambda e: e.scalar_tensor_tensor(out=out, in0=in0, scalar=scalar, in1=in1, op0=op0, op1=op1), rd, [out])

        def cp(out, in_, eng="dve"):
            if eng == "act":
                P.op("act", lambda e: e.copy(out, in_), [in_], [out])
            else:
                P.op(eng, lambda e: e.tensor_copy(out, in_), [in_], [out])

        def memset(ap, val, eng="pool"):
            P.op(eng, lambda e: e.memset(ap, val), [], [ap])

        def dma(out, in_, eng="sp"):
            P.op(eng, lambda e: e.dma_start(out=out, in_=in_), [in_], [out], dma=True)

        cst = A.f32(NC)
        C = lambda n: cst[:, _C[n][0]:_C[n][0] + _C[n][1]]
        msk = A.f32(256).bitcast(I32)
        vecs = A.f32(NV)

        def V(key, lo=0, n=None):
            o, w = _V[key]
            n = w - lo if n is None else n
            return vecs[:, o + lo:o + lo + n]

        rope = A.bf16(2 * T)
        ropecos = rope[:, 0:T]
        ropesin = rope[:, T:2 * T]
        identb = A.bf16(128)
        mod = A.f32(DEPTH * 96)
        mod1p = A.f32(DEPTH * 96)
        lbt = A.f32(DEPTH * 4 * 3)
        csil = A.f32(16)
        xT = A.f32(8 * NT)
        xv = xT.rearrange("p (c t) -> p c t", t=NT)
        mix_rf = A.bf16(4 * NT)
        mix_rf_v = mix_rf.rearrange("p (c t) -> p c t", t=NT)
        cst_f = [A.f32(512) for _ in range(2)]
        cst_b = [A.bf16(512) for _ in range(2)]
        slab = [A.bf16(1024) for _ in range(3)]
        slab_i = [0]
        Rj = [A.bf16(256) for _ in range(2)]

        def MOD(l, g, dc, j):
            o = l * 96 + (g * 8 + dc) * 2 + j
            return mod[:, o:o + 1]

        def MOD1P(l, g, dc, j):
            o = l * 96 + (g * 8 + dc) * 2 + j
            return mod1p[:, o:o + 1]

        def LB(l, d, fc, w):
            o = ((l * 2 + d) * 2 + fc) * 3 + w
            return lbt[:, o:o + 1]

        dma(cst, cst_d)
        dma(msk, msk_d)
        dma(vecs, vecs_d)
        dma(rope, rope_d)
        for dc in range(8):
            dma(xv[:, dc, :], xT0[:, dc * NT:(dc + 1) * NT])
        cp(identb, C("ident"), "dve")

        cast_list = []
        for l in range(n_layers):
            for s in range(NSLAB):
                r0 = (l * NSLAB + s) * 128
                for h in range(2):
                    cast_list.append((win_d[r0:r0 + 128, h * 512:(h + 1) * 512], winb[r0:r0 + 128, h * 512:(h + 1) * 512]))
            for h in range(16):
                cast_list.append((wout_d[l * 128:(l + 1) * 128, h * 512:(h + 1) * 512], woutb[l * 128:(l + 1) * 128, h * 512:(h + 1) * 512]))
            for j in range(NJ):
                r0 = (l * NJ + j) * 128
                for h in range(4):
                    cast_list.append((wup_d[r0:r0 + 128, h * 512:(h + 1) * 512], wupb[r0:r0 + 128, h * 512:(h + 1) * 512]))
            for h in range(NJ * 2):
                cast_list.append((wdown_d[l * 128:(l + 1) * 128, h * 512:(h + 1) * 512], wdownb[l * 128:(l + 1) * 128, h * 512:(h + 1) * 512]))
        import os
        ncast = min(len(cast_list), int(os.environ.get('PRO_CAST', '100000')))

        def cast_load(i):
            dma(cst_f[i % 2], cast_list[i][0], eng="pool")

        def cast_do(i):
            cp(cst_b[i % 2], cst_f[i % 2], "pool")
            dma(cast_list[i][1], cst_b[i % 2], eng="pool")

        if ncast > 0:
            cast_load(0)
        for i in range(ncast):
            if i + 1 < ncast:
                cast_load(i + 1)
            cast_do(i)

        act(csil, V("cc"), AF.Silu)
        csv = csil.rearrange("p (k j) -> p k j", j=2)

        def adaln(l):
            A.push()
            wst = [A.f32(1024) for _ in range(3)]
            for m in range(48):
                w = wst[m % 3]
                r0 = (l * 48 + m) * 128
                dma(w, wada_d[r0:r0 + 128, :])
                wv = w.rearrange("p (k c) -> p k c", c=128)
                for k in range(8):
                    mm(bank(7, 2, m * 2), wv[:, k, :], csv[:, k, :], start=(k == 0), stop=(k == 7))
            mv = mod[:, l * 96:(l + 1) * 96].rearrange("p (m j) -> p m j", j=2)
            if os.environ.get('PRO_ADA', '1') == '2':
                cp(mod[:, l * 96:(l + 1) * 96], bank(7, 96), "dve")
            else:
                pv_ = bank(7, 96).rearrange("p (m j) -> p m j", j=2)
                for j_ in range(2):
                    tt(mv[:, :, j_], pv_[:, :, j_], V(("bada", l)), ALU.add)
            ts(mod1p[:, l * 96:(l + 1) * 96], mod[:, l * 96:(l + 1) * 96], 1.0, None, ALU.add)
            A.pop()

        adaln(0)
        if dbg:
            dma(dbg_t["mod"], mod)
        hl = V("hglb")
        lbv = lbt.rearrange("p (l d f w) -> p l d f w", l=DEPTH, d=2, f=2)
        memset(lbt, 0.0, "dve")
        if n_layers > 1:
            A.push()
            tmp4 = A.f32(4)
            tt(tmp4, hl[:, 4:8], hl[:, 0:4], ALU.subtract)
            act(tmp4, tmp4, AF.Sigmoid)
            cp(lbv[:, 1, :, :, 0], tmp4.rearrange("p (d f) -> p d f", d=2), "dve")
            A.pop()
        for l in range(n_layers):
            ts(lbv[:, l, :, :, 1], lbv[:, l, :, :, 0], -1.0, 1.0, ALU.mult, ALU.add)
            ts(lbv[:, l, :, :, 2], lbv[:, l, :, :, 0], 1.0, -1.0, ALU.mult, ALU.add)

        def load_slab(l, name):
            s = slab[slab_i[0] % 3]
            slab_i[0] += 1
            r0 = (l * NSLAB + SLAB_IDX[name]) * 128
            dma(s, winb[r0:r0 + 128, :])
            return s.rearrange("p (k c) -> p k c", c=128)

        def modulate(dst, l, t0, n, j, gsc, gsh, dst_off=0):
            for dc in range(8):
                o = dst[:, dc, dst_off:dst_off + n]
                if dc % 2 == 0:
                    act(o, xv[:, dc, t0:t0 + n], AF.Identity, scale=MOD1P(l, gsc, dc, j), bias=MOD(l, gsh, dc, j))
                else:
                    ts(o, xv[:, dc, t0:t0 + n], MOD1P(l, gsc, dc, j), MOD(l, gsh, dc, j), ALU.mult, ALU.add)

        def proj_fm(ps, wsl, hT, n):
            for k in range(8):
                mm(ps, wsl[:, k, :], hT[:, k, 0:n], start=(k == 0), stop=(k == 7))

        def layer_norm(l, t0, n, gkey, bkey, scratch):
            sq, m_sb, r_sb, tb = scratch
            pm = bank(5, n)
            pq = bank(6, n)
            for dc in range(8):
                mm(pm, C("ones1024"), xv[:, dc, t0:t0 + n], start=(dc == 0), stop=(dc == 7))
            for dc in range(8):
                s = sq[dc % 2][:, 0:n]
                act(s, xv[:, dc, t0:t0 + n], AF.Square)
                mm(pq, C("ones1024"), s, start=(dc == 0), stop=(dc == 7))
            cp(m_sb[:, 0:n], pm, "act")
            tt(r_sb[:, 0:n], m_sb[:, 0:n], m_sb[:, 0:n], ALU.mult)
            tt(r_sb[:, 0:n], pq, r_sb[:, 0:n], ALU.subtract)
            act(r_sb[:, 0:n], r_sb[:, 0:n], AF.Ln, bias=EPS)
            act(r_sb[:, 0:n], r_sb[:, 0:n], AF.Exp, scale=-0.5)
            for dc in range(8):
                t = tb[dc % 2][:, 0:n]
                tt(t, xv[:, dc, t0:t0 + n], m_sb[:, 0:n], ALU.subtract)
                tt(t, t, r_sb[:, 0:n], ALU.mult)
                act(xv[:, dc, t0:t0 + n], t, AF.Identity, scale=V((gkey, l), dc, 1), bias=V((bkey, l), dc, 1))

        def scan_op(out, d0, d1_):
            P.op("dve", lambda e: e.tensor_tensor_scan(out=out, data0=d0, data1=d1_, initial=0.0,
                                                       op0=ALU.mult, op1=ALU.add), [d0, d1_], [out])

        def pred_copy(out, mask, data):
            P.op("dve", lambda e: e.copy_predicated(out=out, mask=mask, data=data), [mask, data, out], [out])

        def hgrn_phase(l, need_ctx):
            A.push()
            o_f = A.f32(2 * NT)
            o_fv = o_f.rearrange("p (c t) -> p c t", t=NT)
            gate = A.bf16(2 * NT)
            gatev = gate.rearrange("p (c t) -> p c t", t=NT)
            v_tm = A.bf16(18 * 256)
            v_tmv = v_tm.rearrange("p (i c) -> p i c", c=256)
            hTc = A.bf16(8 * 512).rearrange("p (k t) -> p k t", t=512)
            qf = [A.f32(512) for _ in range(2)]
            rr = [A.f32(512) for _ in range(2)]
            kkf = [A.f32(512) for _ in range(2)]
            cum = [A.f32(512) for _ in range(2)]
            d1 = [A.f32(512) for _ in range(2)]
            d4 = rr
            e3 = [A.f32(512) for _ in range(2)]
            Qt = [A.bf16(512) for _ in range(2)]
            Kt = [[A.bf16(512) for _ in range(2)] for _ in range(2)]
            Qh = [A.bf16(512) for _ in range(2)]
            Kh = [A.bf16(512) for _ in range(2)]
            osum = d1
            scT = [[A.bf16(128) for _ in range(2)] for _ in range(2)]
            Khtm = [[A.bf16(128) for _ in range(2)] for _ in range(2)]
            S = [A.f32(64) for _ in range(2)]
            Sbf = [[[A.bf16(64) for _ in range(2)] for _ in range(2)] for _ in range(2)]
            for a_ in scT + Kt + Khtm + [x_ for y_ in Sbf for x_ in y_]:
                for b_ in a_:
                    memset(b_, 0.0, "dve")
            for dirn in ((0, 1) if os.environ.get('HG_DIRS', '2') == '2' else (0,)):
                dn = "hff" if dirn == 0 else "hfb"
                mid, last = (31, 63) if dirn == 0 else (32, 0)
                for fc in range(2):
                    memset(S[fc], 0.0, "dve")
                    for hh in range(2):
                        memset(scT[fc][hh], 0.0, "dve")
                chunks = CH_ALL if dirn == 0 else [CH_ALL[0]] + CH_LAT[::-1]
                chunks = chunks[:int(os.environ.get('HG_CH', '9'))]
                TS = int(os.environ.get('HG_TSTEP', '9'))
                for (t0, n, j) in chunks:
                    is_ctx = (j == 1)
                    do_out = need_ctx or not is_ctx
                    ntile = n // 128
                    nch = n // 64
                    modulate(hTc, l, t0, n, j, 1, 0)
                    if dirn == 0:
                        for ti in range(ntile):
                            pv = bank(7, 256, 256)
                            for hh in range(2):
                                w = load_slab(l, "hi%d" % hh)
                                for k in range(8):
                                    mm(pv[:, hh * 128:(hh + 1) * 128], hTc[:, k, ti * 128:(ti + 1) * 128], w[:, k, :],
                                       start=(k == 0), stop=(k == 7))
                            cp(v_tmv[:, t0 // 128 + ti, :], pv, "act")
                    for fc in range(2):
                        proj_fm(bank(fc, n), load_slab(l, "hq%d" % fc), hTc, n)
                        act(qf[fc][:, 0:n], bank(fc, n), AF.Silu)
                    if dirn == 0 and do_out:
                        for fc in range(2):
                            proj_fm(bank(fc, n), load_slab(l, "hg%d" % fc), hTc, n)
                            act(gatev[:, fc, t0:t0 + n], bank(fc, n), AF.Silu)
                    for fc in range(2):
                        proj_fm(bank(2 + fc, n), load_slab(l, "%s%d" % (dn, fc)), hTc, n)
                    for fc in range(2):
                        act(rr[fc][:, 0:n], bank(2 + fc, n), AF.Sigmoid)
                    for fc in range(2):
                        ts(kkf[fc][:, 0:n], rr[fc][:, 0:n], LB(l, dirn, fc, 2), LB(l, dirn, fc, 1), ALU.mult, ALU.add)
                        ts(rr[fc][:, 0:n], rr[fc][:, 0:n], LB(l, dirn, fc, 1), LB(l, dirn, fc, 0), ALU.mult, ALU.add)
                    for fc in range(2):
                        act(rr[fc][:, 0:n], rr[fc][:, 0:n], AF.Ln)
                    sm = C("scanmask")[:, 0:n]
                    for fc in range(2):
                        if dirn == 0:
                            scan_op(cum[fc][:, 0:n], sm, rr[fc][:, 0:n])
                        else:
                            scan_op(cum[fc][:, 0:n][:, ::-1], sm, rr[fc][:, 0:n][:, ::-1])
                        c3 = cum[fc][:, 0:n].rearrange("p (c t) -> p c t", t=64)
                        tt(d1[fc][:, 0:n].rearrange("p (c t) -> p c t", t=64), c3,
                           c3[:, :, mid:mid + 1].to_broadcast([128, nch, 64]), ALU.subtract)
                        tt(d4[fc][:, 0:n].rearrange("p (c t) -> p c t", t=64),
                           c3[:, :, last:last + 1].to_broadcast([128, nch, 64]), c3, ALU.subtract)
                    for fc in range(2):
                        act(e3[fc][:, 0:n], cum[fc][:, 0:n], AF.Exp)
                        act(d4[fc][:, 0:n], d4[fc][:, 0:n], AF.Exp)
                        if do_out:
                            act(cum[fc][:, 0:n], d1[fc][:, 0:n], AF.Exp)
                            act(d1[fc][:, 0:n], d1[fc][:, 0:n], AF.Exp, scale=-1.0)
                    for fc in range(2):
                        tt(Qh[fc][:, 0:n], qf[fc][:, 0:n], e3[fc][:, 0:n], ALU.mult)
                        tt(Kh[fc][:, 0:n], kkf[fc][:, 0:n], d4[fc][:, 0:n], ALU.mult, eng="dve")
                        if do_out:
                            tt(Qt[fc][:, 0:n], qf[fc][:, 0:n], cum[fc][:, 0:n], ALU.mult)
                            for hh in range(2):
                                hs = slice(hh * 64, (hh + 1) * 64)
                                tt(Kt[fc][hh][hs, 0:n], kkf[fc][hs, 0:n], d1[fc][hs, 0:n], ALU.mult, eng="dve")
                    tiles = list(range(ntile)) if dirn == 0 else list(range(ntile))[::-1]
                    order = (0, 1) if dirn == 0 else (1, 0)
                    for ti in (tiles if os.environ.get('HG_TILES', '1') == '1' else []):
                        a = ti * 128
                        gi = t0 // 128 + ti
                        for fc in range(2):
                            pt = bank(4 if fc == 0 else 0, 128, 0)
                            mm(pt, Kh[fc][:, a:a + 128], identb)
                            for cc in range(2):
                                ts(Khtm[fc][cc], pt, C("hmask")[:, cc:cc + 1], None, ALU.mult)
                            po = bank(6 if fc == 0 else 2, 128, 0)
                            if TS < 2:
                                continue
                            if do_out:
                                for hh in range(2):
                                    hs = slice(hh * 64, (hh + 1) * 64)
                                    psc = bank(5 if fc == 0 else 1, 128, hh * 128)
                                    mm(psc, Kt[fc][hh][:, a:a + 128], Qt[fc][:, a:a + 128])
                                    pred_copy(scT[fc][hh], msk[:, dirn * 128:(dirn + 1) * 128], psc)
                            if TS < 3:
                                continue
                            for ci, cc in enumerate(order):
                                if do_out:
                                    for hh in range(2):
                                        hs = slice(hh * 64, (hh + 1) * 64)
                                        cp(Sbf[fc][ci][hh][hs, :], S[fc][hs, :], "act")
                                pu = bank(7 if fc == 0 else 3, 64, ci * 64)
                                for hh in range(2):
                                    hs = slice(hh * 64, (hh + 1) * 64)
                                    vcol = (fc * 2 + hh) * 64
                                    mm(pu[hs, :], Khtm[fc][cc][:, hs], v_tmv[:, gi, vcol:vcol + 64])
                                dcol = a + cc * 64 + last
                                stt(S[fc], S[fc], e3[fc][:, dcol:dcol + 1], pu, ALU.mult, ALU.add)
                            if TS < 4:
                                continue
                            if do_out:
                                for hh in range(2):
                                    hs = slice(hh * 64, (hh + 1) * 64)
                                    vcol = (fc * 2 + hh) * 64
                                    mm(po[hs, :], v_tmv[:, gi, vcol:vcol + 64], scT[fc][hh], start=True, stop=False)
                                    for ci, cc in enumerate(order):
                                        mm(po[hs, cc * 64:(cc + 1) * 64], Sbf[fc][ci][hh],
                                           Qh[fc][:, a + cc * 64:a + (cc + 1) * 64], start=False, stop=(ci == 1))
                                if dirn == 0:
                                    cp(o_fv[:, fc, t0 + a:t0 + a + 128], po, "act")
                                else:
                                    tt(osum[fc][:, a:a + 128], po, o_fv[:, fc, t0 + a:t0 + a + 128], ALU.add)
                    if dirn == 1 and do_out:
                        for fc in range(2):
                            act(qf[fc][:, 0:n], osum[fc][:, 0:n], AF.Square)
                            mm(bank(fc, n), C("blk64"), qf[fc][:, 0:n])
                        for fc in range(2):
                            act(qf[fc][:, 0:n], bank(fc, n), AF.Ln, bias=EPS)
                        for fc in range(2):
                            act(qf[fc][:, 0:n], qf[fc][:, 0:n], AF.Exp, scale=-0.5)
                            stt(osum[fc][:, 0:n], osum[fc][:, 0:n], V(("hgn", l)), qf[fc][:, 0:n], ALU.mult, ALU.mult)
                            tt(mix_rf_v[:, fc, t0:t0 + n], osum[fc][:, 0:n], gatev[:, fc, t0:t0 + n], ALU.mult)
            A.pop()

        def fourier_phase(l, need_ctx):
            A.push()
            hTc = A.bf16(8 * 512).rearrange("p (k t) -> p k t", t=512)
            uTb = [A.bf16(512) for _ in range(2)]
            PQ = A.bf16(18 * 2 * 256).rearrange("p (i j c) -> p i j c", j=2, c=256)
            tabs = [[A.bf16(16 * 256).rearrange("p (i c) -> p i c", c=256) for _ in range(2)] for _ in range(2)]
            ftw = V(("ftw", l)).rearrange("p (j d) -> p j d", d=64)
            for j in range(2):
                memset(Rj[j], 0.0, "dve")
                pa = bank(0, 64)
                pb = bank(1, 64)
                mm(pa, C("c64"), ftw[:, j, :])
                mm(pb, C("s64"), ftw[:, j, :])
                for g in range(2):
                    gs = slice(g * 64, (g + 1) * 64)
                    cp(Rj[j][gs, g * 64:(g + 1) * 64], pa[gs, :], "dve")
                    cp(Rj[j][gs, 128 + g * 64:128 + (g + 1) * 64], pb[gs, :], "dve")
            chunks = CH_ALL if need_ctx else CH_LAT
            for (t0, n, jm) in chunks:
                modulate(hTc, l, t0, n, jm, 1, 0)
                for j in range(2):
                    proj_fm(bank(j, n), load_slab(l, "ft%d" % j), hTc, n)
                    cp(uTb[j][:, 0:n], bank(j, n), "act")
                for ti in range(n // 128):
                    for j in range(2):
                        pp = bank(2 + j, 256)
                        mm(pp, uTb[j][:, ti * 128:(ti + 1) * 128], Rj[j])
                        cp(PQ[:, t0 // 128 + ti, j, :], pp, "dve" if j == 0 else "act")
            for nn in range(8):
                tb = tabs[nn % 2]
                for cs in range(2):
                    r0 = (nn * 2 + cs) * 128
                    dma(tb[cs], dftl_d[r0:r0 + 128, :].rearrange("p (i c) -> p i c", c=256))
                for j in range(2):
                    pf = bank(4 + j, 256)
                    for ti in range(16):
                        for cs in range(2):
                            mm(pf, PQ[:, 2 + ti, j, cs * 128:(cs + 1) * 128], tb[cs][:, ti, :],
                               start=(ti == 0 and cs == 0), stop=(ti == 15 and cs == 1))
                    cp(mix_rf_v[:, 2 + j, CT + nn * 256:CT + (nn + 1) * 256], pf, "act" if j == 0 else "dve")
            if need_ctx:
                tb = tabs[0]
                for cs in range(2):
                    dma(tb[cs][:, 0:2, :], dftc_d[cs * 128:(cs + 1) * 128, :].rearrange("p (i c) -> p i c", c=256))
                for j in range(2):
                    pf = bank(4 + j, 256)
                    for ti in range(2):
                        for cs in range(2):
                            mm(pf, PQ[:, ti, j, cs * 128:(cs + 1) * 128], tb[cs][:, ti, :],
                               start=(ti == 0 and cs == 0), stop=(ti == 1 and cs == 1))
                    cp(mix_rf_v[:, 2 + j, 0:CT], pf, "act" if j == 0 else "dve")
            if l == 0 and n_layers > 1:
                adaln(1)
            A.pop()

        def attention_phase(l, need_ctx, mix_a):
            A.push()
            qT = A.bf16(4 * NT).rearrange("p (c t) -> p c t", t=NT)
            kTp = [[A.bf16(NT) for _ in range(2)] for _ in range(2)]
            Vaug = A.bf16(18 * 2 * 192).rearrange("p (i h c) -> p i h c", h=2, c=192)
            A.push()
            hTc = A.bf16(8 * 512).rearrange("p (k t) -> p k t", t=512)
            sq = A.f32(512)
            rs = A.f32(512)
            qn = A.f32(512)
            t1 = sq
            A.pop()
            PT = [A.bf16(1024) for _ in range(2)]
            o_sb = [A.f32(512) for _ in range(2)]
            r_row = {0: A.f32(512), 64: A.f32(512)}
            for a_ in kTp:
                for b_ in a_:
                    memset(b_, 0.0, "dve")
            memset(Vaug, 1.0, "dve")

            def qk_chunk(name, gkey, dst, t0, n, rope_on):
                ps = bank(0, n)
                proj_fm(ps, load_slab(l, name), hTc, n)
                act(sq[:, 0:n], ps, AF.Square)
                pm = bank(1, n)
                mm(pm, C("blk64"), sq[:, 0:n])
                act(rs[:, 0:n], pm, AF.Ln, bias=EPS)
                act(rs[:, 0:n], rs[:, 0:n], AF.Exp, scale=-0.5)
                if not rope_on:
                    for (d_, hs) in dst:
                        stt(d_[hs, :], ps[hs, :], V((gkey, l))[hs, :], rs[hs, 0:n], ALU.mult, ALU.mult)
                    return
                stt(qn[:, 0:n], ps, V((gkey, l)), rs[:, 0:n], ALU.mult, ALU.mult)
                pr = bank(2, n)
                mm(pr, C("rperm"), qn[:, 0:n])
                lt = t0 - CT
                tt(t1[:, 0:n], qn[:, 0:n], ropecos[:, lt:lt + n], ALU.mult)
                tt(qn[:, 0:n], pr, ropesin[:, lt:lt + n], ALU.mult)
                for (d_, hs) in dst:
                    tt(d_[hs, :], t1[hs, 0:n], qn[hs, 0:n], ALU.add)

            for (t0, n, jm) in CH_ALL:
                is_ctx = (jm == 1)
                modulate(hTc, l, t0, n, jm, 1, 0)
                w = load_slab(l, "v")
                for ti in range(n // 128):
                    pv = bank(3, 128)
                    for k in range(8):
                        mm(pv, hTc[:, k, ti * 128:(ti + 1) * 128], w[:, k, :], start=(k == 0), stop=(k == 7))
                    cp(Vaug[:, t0 // 128 + ti, :, 64:128], pv.rearrange("p (h c) -> p h c", c=64), "act")
                lo_, hi_ = slice(0, 64), slice(64, 128)
                qk_chunk("ka", "kg", [(kTp[0][0][:, t0:t0 + n], lo_), (kTp[1][1][:, t0:t0 + n], hi_)], t0, n, not is_ctx)
                qk_chunk("kb", "kg", [(kTp[1][0][:, t0:t0 + n], lo_), (kTp[0][1][:, t0:t0 + n], hi_)], t0, n, not is_ctx)
                if (not is_ctx) or need_ctx:
                    for qc in range(4):
                        qk_chunk("q%d" % qc, "qg", [(qT[:, qc, t0:t0 + n], slice(0, 128))], t0, n, not is_ctx)

            def attend(q0, nq, ktiles):
                nsub = (nq + 511) // 512
                for h in range(8):
                    qc, hh, kvh = h // 2, h % 2, h // 4
                    hs = slice(hh * 64, (hh + 1) * 64)
                    def scores(ki):
                        kt = ktiles[ki]
                        pst_ = pst[:, (ki % 2) * 1024:(ki % 2) * 1024 + nq]
                        for s_ in range(nsub):
                            w_ = min(512, nq - s_ * 512)
                            mm(pst_[:, s_ * 512:s_ * 512 + w_], kTp[kvh][hh][:, kt * 128:(kt + 1) * 128],
                               qT[:, qc, q0 + s_ * 512:q0 + s_ * 512 + w_])

                    scores(0)
                    for ki, kt in enumerate(ktiles):
                        if ki + 1 < len(ktiles):
                            scores(ki + 1)
                        pst_ = pst[:, (ki % 2) * 1024:(ki % 2) * 1024 + nq]
                        p_ = PT[ki % 2][:, 0:nq]
                        act(p_, pst_, AF.Exp, scale=0.125)
                        vl = Vaug[:, kt, kvh, 64:192] if hh == 0 else Vaug[:, kt, kvh, 0:128]
                        for s_ in range(nsub):
                            w_ = min(512, nq - s_ * 512)
                            mm(bank(4 + s_, w_), vl, p_[:, s_ * 512:s_ * 512 + w_],
                               start=(ki == 0), stop=(ki == len(ktiles) - 1))
                    for s_ in range(nsub):
                        w_ = min(512, nq - s_ * 512)
                        ob = o_sb[s_][:, 0:w_]
                        cp(ob, bank(4 + s_, w_), "act")
                        drow = 64 if hh == 0 else 0
                        rr_ = r_row[drow]
                        P.op("dve", (lambda o_, i_: lambda e: e.reciprocal(o_, i_))(
                            rr_[drow:drow + 1, 0:w_], ob[drow:drow + 1, :]), [ob[drow:drow + 1, :]], [rr_[drow:drow + 1, 0:w_]])
                        pb = bank(6, w_)
                        mm(pb, C("ones"), rr_[:, 0:w_])
                        tt(mix_a[hs, qc, q0 + s_ * 512:q0 + s_ * 512 + w_], ob[hs, :], pb[hs, :], ALU.mult)

            memset(r_row[0], 0.0, "dve")
            memset(r_row[64], 0.0, "dve")
            for half in range(2):
                attend(CT + half * 1024, 1024, list(range(18)))
            if need_ctx:
                attend(0, 256, [0, 1])
            A.pop()

        def outproj_phase(l, need_ctx, mix_a):
            A.push()
            sq = [A.f32(512) for _ in range(2)]
            m_sb = A.f32(512)
            r_sb = A.f32(512)
            tb = [A.f32(512) for _ in range(2)]
            wo = A.bf16(8192).rearrange("p (k c) -> p k c", c=1024)
            dma(wo, woutb[l * 128:(l + 1) * 128, :].rearrange("p (k c) -> p k c", c=1024))
            chunks = CH_ALL if need_ctx else CH_LAT
            for (t0, n, jm) in chunks:
                for dc in range(8):
                    ps = bank(dc % 4, n)
                    for mc in range(8):
                        src = mix_a[:, mc, t0:t0 + n] if mc < 4 else mix_rf_v[:, mc - 4, t0:t0 + n]
                        mm(ps, wo[:, mc, dc * 128:(dc + 1) * 128], src, start=(mc == 0), stop=(mc == 7))
                    t = tb[dc % 2][:, 0:n]
                    act(t, ps, AF.Identity, scale=MOD(l, 2, dc, jm))
                    stt(xv[:, dc, t0:t0 + n], xv[:, dc, t0:t0 + n], ALPHA, t, ALU.mult, ALU.add)
                layer_norm(l, t0, n, "ln1g", "ln1b", (sq, m_sb, r_sb, tb))
            A.pop()

        def ffn_phase(l, need_ctx, last):
            A.push()
            sq = [A.f32(512) for _ in range(2)]
            m_sb = A.f32(512)
            r_sb = A.f32(512)
            tb = [A.f32(512) for _ in range(2)]
            wds = [A.bf16(NJ * 128).rearrange("p (j c) -> p j c", c=128) for _ in range(3)]
            NCM = 412
            h2 = A.bf16(8 * NCM).rearrange("p (k t) -> p k t", t=NCM)
            actb = A.bf16(NJ * 410).rearrange("p (j t) -> p j t", t=410)
            NWU = 5
            wu = [A.bf16(2048).rearrange("p (k c) -> p k c", c=256) for _ in range(NWU)]
            cA = [A.f32(410) for _ in range(2)]
            cG = [A.f32(410) for _ in range(2)]
            sG = [A.bf16(410) for _ in range(2)]
            hprev = A.bf16(8).rearrange("p (k t) -> p k t", t=1)
            cw = V(("convw", l)).rearrange("p (c t) -> p c t", t=3)
            cb = V(("convb", l))
            seqs = ([(0, CT, 1, FFN_CTX)] if need_ctx else []) + [(CT, T, 0, FFN_LAT)]
            ui = 0
            di = 0
            for (s0, slen, jm, chl) in seqs:
                for (t0, n) in chl:
                    lo = t0 - 1
                    hi = t0 + n + 1
                    a1 = min(hi, s0 + slen)
                    if lo < s0:
                        memset(h2[:, :, 0:1], 0.0, "dve")
                    else:
                        cp(h2[:, :, 0:1], hprev, "dve")
                    if hi > s0 + slen:
                        memset(h2[:, :, n + 1:n + 2], 0.0, "dve")
                    modulate(h2, l, t0, a1 - t0, jm, 4, 3, dst_off=1)
                    cp(hprev, h2[:, :, n:n + 1], "dve")
                    for j in range(NJ):
                        w = wu[ui % NWU]
                        ui += 1
                        r0 = (l * NJ + j) * 128
                        dma(w, wupb[r0:r0 + 128, :].rearrange("p (k c) -> p k c", c=256))
                        pa = bank(j % 2, n + 2)
                        pg = bank(2 + j % 2, n + 2)
                        for k in range(8):
                            mm(pa, w[:, k, 0:128], h2[:, k, 0:n + 2], start=(k == 0), stop=(k == 7))
                        for k in range(8):
                            mm(pg, w[:, k, 128:256], h2[:, k, 0:n + 2], start=(k == 0), stop=(k == 7))
                        ca = cA[j % 2][:, 0:n]
                        cg = cG[j % 2][:, 0:n]
                        sg = sG[j % 2][:, 0:n]
                        for (pp, cc_, fi) in ((pa, ca, j), (pg, cg, NJ + j)):
                            act(cc_, pp[:, 1:n + 1], AF.Identity, scale=cw[:, fi, 1:2], bias=cb[:, fi:fi + 1])
                            stt(cc_, pp[:, 0:n], cw[:, fi, 0:1], cc_, ALU.mult, ALU.add)
                            stt(cc_, pp[:, 2:n + 2], cw[:, fi, 2:3], cc_, ALU.mult, ALU.add)
                        act(sg, cg, AF.Silu)
                        tt(actb[:, j, 0:n], ca, sg, ALU.mult)
                    for dc in range(8):
                        ps = bank(4 + dc % 2, n)
                        wd = wds[di % 3]
                        di += 1
                        dma(wd, wdownb[l * 128:(l + 1) * 128, dc * NJ * 128:(dc + 1) * NJ * 128].rearrange("p (j c) -> p j c", c=128))
                        for j in range(NJ):
                            mm(ps, wd[:, j, :], actb[:, j, 0:n], start=(j == 0), stop=(j == NJ - 1))
                        t = tb[dc % 2][:, 0:n]
                        act(t, ps, AF.Identity, scale=MOD(l, 5, dc, jm))
                        stt(xv[:, dc, t0:t0 + n], xv[:, dc, t0:t0 + n], ALPHA, t, ALU.mult, ALU.add)
                    layer_norm(l, t0, n, "ln2g", "ln2b", (sq, m_sb, r_sb, tb))
                    if last and jm == 0:
                        for dc in range(8):
                            dma(yT[:, dc * T + (t0 - CT):dc * T + (t0 - CT) + n], xv[:, dc, t0:t0 + n])
            A.pop()

        def dump_x(key):
            if dbg and key in dbg_t:
                dma(dbg_t[key], xT)

        stop = getattr(build_program, "stop", "")
        for l in range(n_layers):
            need_ctx = l < DEPTH - 1
            if stop == "prologue":
                break
            if "nohgrn" not in stop:
                hgrn_phase(l, need_ctx)
            if stop == "hgrn":
                dma(dbg_t["mix%d" % l][:, 4 * NT:8 * NT], mix_rf)
                break
            if "nofourier" not in stop:
                fourier_phase(l, need_ctx)
            if stop.startswith("fourier"):
                dma(dbg_t["mix%d" % l][:, 4 * NT:8 * NT], mix_rf)
                break
            A.push()
            mix_a = A.bf16(4 * NT).rearrange("p (c t) -> p c t", t=NT)
            attention_phase(l, need_ctx, mix_a)
            if stop.startswith("attn"):
                dma(dbg_t["mix%d" % l][:, 0:4 * NT].rearrange("p (c t) -> p c t", t=NT), mix_a)
                break
            if dbg:
                mk = dbg_t["mix%d" % l]
                dma(mk[:, 0:4 * NT].rearrange("p (c t) -> p c t", t=NT), mix_a)
                dma(mk[:, 4 * NT:8 * NT], mix_rf)
            outproj_phase(l, need_ctx, mix_a)
            A.pop()
            dump_x("x1_%d" % l)
            ffn_phase(l, need_ctx, last=(l == n_layers - 1))
            dump_x("x_%d" % l)
        P.emit()
        build_program.stats = P.stats
        build_program.arena_peak = A.peak
    return nc


_CACHE = {}


def kernel(**inputs):
    maps = host_layout(inputs)
    if "nc" not in _CACHE:
        _CACHE["nc"] = build_program()
    nc = _CACHE["nc"]
    res = run_bass_kernel_spmd(nc, maps, core_ids=list(range(len(maps))))
    outs = []
    for r in res.results:
        y = np.asarray(r["yT"]).reshape(128, 8, T)
        outs.append(np.ascontiguousarray(y.transpose(2, 1, 0)).reshape(T, D))
    return np.stack(outs, axis=0).astype(np.float32)
```

```python
import numpy as np
import concourse.bass as bass
import concourse.mybir as mybir
from concourse.bass_utils import run_bass_kernel_spmd

F32 = mybir.dt.float32
BF16 = mybir.dt.bfloat16
I32 = mybir.dt.int32
AF = mybir.ActivationFunctionType
ALU = mybir.AluOpType
AX = mybir.AxisListType

_ESZ = {F32: 4, BF16: 2, I32: 4}


def _acc(ap):
    t = ap.tensor
    name = t.name
    esz = _ESZ[ap.dtype]
    pat = ap.ap
    off = int(ap.offset)
    sp = str(ap.space)
    if "DRAM" in sp.upper() or "HBM" in sp.upper():
        lo = off
        hi = off
        for st, cnt in pat:
            if st >= 0:
                hi += st * (cnt - 1)
            else:
                lo += st * (cnt - 1)
        return ("d:" + name, 0, 1, lo * esz, (hi + 1) * esz)
    row = pat[0][0]
    p0 = off // row
    col = off % row
    p1 = p0 + pat[0][1]
    lo = col
    hi = col
    for st, cnt in pat[1:]:
        if st >= 0:
            hi += st * (cnt - 1)
        else:
            lo += st * (cnt - 1)
    if "PSUM" in sp.upper():
        b0 = (lo * esz) // 2048
        b1 = ((hi + 1) * esz + 2047) // 2048
        return ("p:" + name, 0, 128, b0 * 2048, b1 * 2048)
    key = "s:" + name
    return (key, p0, p1, lo * esz, (hi + 1) * esz)


class Prog:
    ENGS = ("pe", "act", "dve", "pool", "sp")

    def __init__(self, nc, n_dma_sems=20):
        self.nc = nc
        self.ops = []
        self.deps = []
        self.recs = {}
        self.n_dma_sems = n_dma_sems

    def op(self, eng, fn, reads=(), writes=(), dma=False):
        oid = len(self.ops)
        deps = set()
        for ap in reads:
            k, p0, p1, lo, hi = _acc(ap)
            lst = self.recs.setdefault(k, [])
            for r in lst:
                if r[4] == "w" and r[0] < p1 and p0 < r[1] and r[2] < hi and lo < r[3]:
                    deps.add(r[5])
        wr = []
        for ap in writes:
            k, p0, p1, lo, hi = _acc(ap)
            wr.append((k, p0, p1, lo, hi))
            lst = self.recs.setdefault(k, [])
            keep = []
            for r in lst:
                if r[0] < p1 and p0 < r[1] and r[2] < hi and lo < r[3]:
                    deps.add(r[5])
                    if p0 <= r[0] and r[1] <= p1 and lo <= r[2] and r[3] <= hi:
                        continue
                keep.append(r)
            self.recs[k] = keep
        for ap in reads:
            k, p0, p1, lo, hi = _acc(ap)
            lst = self.recs[k]
            if not dma:
                for i, r in enumerate(lst):
                    if r[4] == "r" and r[6] == eng and r[0] == p0 and r[1] == p1 and r[2] == lo and r[3] == hi:
                        lst[i] = [p0, p1, lo, hi, "r", oid, eng]
                        break
                else:
                    lst.append([p0, p1, lo, hi, "r", oid, eng])
            else:
                lst.append([p0, p1, lo, hi, "r", oid, eng])
        for (k, p0, p1, lo, hi) in wr:
            self.recs[k].append([p0, p1, lo, hi, "w", oid, eng])
        deps.discard(oid)
        self.ops.append((eng, fn, dma))
        self.deps.append(deps)
        return oid

    def emit(self):
        nc = self.nc
        ops, deps = self.ops, self.deps
        n = len(ops)
        needed = [False] * n
        for i in range(n):
            ei = ops[i][0]
            for d in deps[i]:
                ed, _, ddma = ops[d]
                if ed == "pe" and ei == "pe":
                    continue
                needed[d] = True
        sig = [None] * n
        cnt = {e: 0 for e in self.ENGS}
        dma_i = 0
        dma_prev = {}
        dma_use = [0] * self.n_dma_sems
        for i in range(n):
            e, _, isd = ops[i]
            if isd:
                if e == "sp":
                    s = dma_prev.get("sp", 0) % 14
                    dma_prev["sp"] = dma_prev.get("sp", 0) + 1
                else:
                    s = 14 + dma_prev.get("pool", 0) % (self.n_dma_sems - 14)
                    dma_prev["pool"] = dma_prev.get("pool", 0) + 1
                dma_use[s] += 1
                sig[i] = ("dma%d" % s, 16 * dma_use[s])
            elif needed[i]:
                cnt[e] += 1
                sig[i] = (e, cnt[e])
        streams = {e: [] for e in self.ENGS}
        waited = {e: {} for e in self.ENGS}
        for i in range(n):
            e, fn, isd = ops[i]
            w = {}
            for d in deps[i]:
                ed = ops[d][0]
                if ed == "pe" and e == "pe" and not ops[d][2]:
                    continue
                s, v = sig[d]
                if v > w.get(s, 0):
                    w[s] = v
            if isd:
                s, v = sig[i]
                if v > 16:
                    w[s] = max(w.get(s, 0), v - 16)
            wl = []
            for s, v in w.items():
                if waited[e].get(s, 0) < v:
                    waited[e][s] = v
                    wl.append((s, v))
            streams[e].append((wl, fn, sig[i], isd))
        self.stats = {e: len(streams[e]) for e in self.ENGS}
        self.stats["signals"] = dict(cnt)
        sem_names = list(self.ENGS) + ["dma%d" % s for s in range(self.n_dma_sems)]
        import contextlib
        with contextlib.ExitStack() as st:
            sems = {nm: st.enter_context(nc.semaphore("sem_" + nm)) for nm in sem_names}
            block = st.enter_context(nc.Block())

            def run(engobj, ename):
                for wl, fn, sg, isd in streams[ename]:
                    for s, v in wl:
                        engobj.wait_ge(sems[s], v)
                    ins = fn(engobj)
                    if sg is not None:
                        ins.then_inc(sems[sg[0]], 16 if isd else 1)
                if ename == "sp":
                    for s in range(self.n_dma_sems):
                        if dma_use[s] > 0:
                            engobj.wait_ge(sems["dma%d" % s], 16 * dma_use[s])

            @block.sync
            def _(e):
                run(e, "sp")

            @block.scalar
            def _(e):
                run(e, "act")

            @block.vector
            def _(e):
                run(e, "dve")

            @block.gpsimd
            def _(e):
                run(e, "pool")

            @block.tensor
            def _(e):
                run(e, "pe")


class Arena:
    def __init__(self, tensor, nwords):
        self.t = tensor
        self.n = nwords
        self.off = 0
        self.marks = []

    def f32(self, cols, parts=128):
        a = self.t[0:parts, self.off:self.off + cols]
        self.off += cols
        assert self.off <= self.n, ("arena overflow", self.off, self.n)
        return a

    def bf16(self, cols, parts=128):
        w = (cols + 1) // 2
        a = self.t[0:parts, self.off:self.off + w].bitcast(BF16)
        self.off += w
        assert self.off <= self.n, ("arena overflow", self.off, self.n)
        return a

    def push(self):
        self.marks.append(self.off)
        self.peak = max(getattr(self, 'peak', 0), self.off)

    def pop(self):
        self.peak = max(getattr(self, 'peak', 0), self.off)
        self.off = self.marks.pop()


D = 1024
T = 2048
CT = 256
NT = CT + T
DEPTH = 2
DFF = 2816
NJ = DFF // 128
ALPHA = (2 * DEPTH) ** 0.25
EPS = 1e-6
GRID_W = 64
ROPE_THETA = 10000.0

WIN_SLABS = ["q0", "q1", "q2", "q3", "ka", "kb", "v", "hq0", "hq1", "hi0", "hi1",
             "hff0", "hff1", "hfb0", "hfb1", "hg0", "hg1", "ft0", "ft1"]
WIN_COL = {"q0": 0, "q1": 128, "q2": 256, "q3": 384, "ka": 512, "v": 640, "hq0": 768, "hq1": 896,
           "hi0": 1024, "hi1": 1152, "hff0": 1280, "hff1": 1408, "hfb0": 1536, "hfb1": 1664,
           "hg0": 1792, "hg1": 1920, "ft0": 2048, "ft1": 2176}
SLAB_IDX = {n: i for i, n in enumerate(WIN_SLABS)}
NSLAB = len(WIN_SLABS)

_V = {}
_off = 0
for _l in range(DEPTH):
    for _n, _w in (("bada", 48), ("qg", 1), ("kg", 1), ("hgn", 1), ("ln1g", 8), ("ln1b", 8),
                   ("ln2g", 8), ("ln2b", 8), ("convw", 132), ("convb", 44), ("ftw", 128)):
        _V[(_n, _l)] = (_off, _w)
        _off += _w
_V["hglb"] = (_off, 8)
_off += 8
_V["cc"] = (_off, 16)
_off += 16
NV = _off

_C = {}
_off = 0
for _n, _w in (("ident", 128), ("blk64", 128), ("ones1024", 128), ("ones", 128), ("rperm", 128),
               ("c64", 128), ("s64", 128), ("scanmask", 512), ("hmask", 2)):
    _C[_n] = (_off, _w)
    _off += _w
NC = _off

CH_ALL = [(0, 256, 1), (256, 512, 0), (768, 512, 0), (1280, 512, 0), (1792, 512, 0)]
CH_LAT = CH_ALL[1:]
FFN_LAT = [(256 + i * 410, 410) for i in range(4)] + [(256 + 1640, 408)]
FFN_CTX = [(0, 256)]


def host_consts():
    c = np.zeros((128, NC), np.float32)
    o = _C["ident"][0]
    c[:, o:o + 128] = np.eye(128, dtype=np.float32)
    o = _C["blk64"][0]
    for h in range(2):
        c[h * 64:(h + 1) * 64, o + h * 64:o + (h + 1) * 64] = 1.0 / 64
    o = _C["ones1024"][0]
    c[:, o:o + 128] = 1.0 / 1024
    o = _C["ones"][0]
    c[:, o:o + 128] = 1.0
    o = _C["rperm"][0]
    sign = np.zeros(128, np.float32)
    for p in range(128):
        w = (p % 64) % 32
        partner = p + 16 if w < 16 else p - 16
        c[partner, o + p] = 1.0
        sign[p] = -1.0 if w < 16 else 1.0
    k = np.arange(64)
    ang = 2 * np.pi * np.outer(k, k) / 64.0
    o = _C["c64"][0]
    o2 = _C["s64"][0]
    for h in range(2):
        c[h * 64:(h + 1) * 64, o + h * 64:o + (h + 1) * 64] = np.cos(ang)
        c[h * 64:(h + 1) * 64, o2 + h * 64:o2 + (h + 1) * 64] = np.sin(ang)
    o = _C["hmask"][0]
    c[0:64, o] = 1.0
    c[64:128, o + 1] = 1.0
    o = _C["scanmask"][0]
    c[:, o:o + 512] = 1.0
    c[:, o:o + 512:64] = 0.0
    s = np.arange(128)[:, None]
    t = np.arange(128)[None, :]
    same = (s // 64) == (t // 64)
    msk = np.concatenate([(same & (s <= t)), (same & (s >= t))], axis=1).astype(np.int32)
    nf = 16
    inv = (ROPE_THETA ** (-np.arange(nf, dtype=np.float32) / nf)).astype(np.float32)
    pos_row = (np.arange(T) // GRID_W).astype(np.float32)
    pos_col = (np.arange(T) % GRID_W).astype(np.float32)
    cosT = np.zeros((128, T), np.float32)
    sinT = np.zeros((128, T), np.float32)
    for p in range(128):
        i = p % 64
        pos = pos_row if i < 32 else pos_col
        a = (pos * inv[i % 16]).astype(np.float32)
        cosT[p] = np.cos(a)
        sinT[p] = np.sin(a) * sign[p]
    rope = np.concatenate([cosT, sinT], axis=1).astype(np.float32)
    import ml_dtypes
    rope = rope.astype(ml_dtypes.bfloat16)
    import ml_dtypes

    def dft_tabs(n):
        k = np.arange(n, dtype=np.float64)
        a = 2 * np.pi * np.outer(k, k) / n
        sc = 1.0 / np.sqrt(n * 64.0)
        return (np.cos(a) * sc), (-np.sin(a) * sc)

    Cl, Sl = dft_tabs(T)

    def lay(m, n):
        nt = n // 128
        nch = n // 256
        r = m.reshape(nt, 128, nch, 256).transpose(2, 1, 0, 3)
        return np.ascontiguousarray(r).astype(ml_dtypes.bfloat16)

    dftl = np.stack([lay(Cl, T), lay(Sl, T)], axis=1)
    Cc, Sc = dft_tabs(CT)
    dftc = np.stack([lay(Cc, CT), lay(Sc, CT)], axis=1)
    return c, msk, rope, dftl, dftc


def host_layout(inp):
    f = lambda a: np.ascontiguousarray(np.asarray(a, dtype=np.float32))
    x, c, ctx, c_ctx = f(inp["x"]), f(inp["c"]), f(inp["ctx"]), f(inp["c_ctx"])
    w_ada, b_ada, w_in = f(inp["w_ada"]), f(inp["b_ada"]), f(inp["w_in"])
    B = x.shape[0]
    sh = {}
    sh["wada"] = np.ascontiguousarray(
        w_ada.reshape(DEPTH, 8, 128, 48, 128).transpose(0, 3, 2, 1, 4)).reshape(DEPTH * 48 * 128, 1024)
    colidx = []
    for nme in WIN_SLABS:
        if nme == "kb":
            colidx.append(np.concatenate([np.arange(576, 640), np.arange(512, 576)]))
        else:
            colidx.append(WIN_COL[nme] + np.arange(128))
    colidx = np.concatenate(colidx)
    win = w_in[:, :, colidx].reshape(DEPTH, 8, 128, NSLAB, 128).transpose(0, 3, 2, 1, 4)
    sh["win"] = np.ascontiguousarray(win).reshape(DEPTH * NSLAB * 128, 1024)
    sh["wout"] = np.ascontiguousarray(
        f(inp["w_out"]).reshape(DEPTH, 8, 128, 1024).transpose(0, 2, 1, 3)).reshape(DEPTH * 128, 8192)
    wup = f(inp["w_up"]).reshape(DEPTH, 8, 128, 2, NJ, 128).transpose(0, 4, 2, 1, 3, 5)
    sh["wup"] = np.ascontiguousarray(wup).reshape(DEPTH * NJ * 128, 2048)
    sh["wdown"] = np.ascontiguousarray(
        f(inp["w_down"]).reshape(DEPTH, NJ, 128, 8, 128).transpose(0, 2, 3, 1, 4)).reshape(DEPTH * 128, NJ * 1024)
    vec = np.zeros((128, NV), np.float32)

    def put(key, arr):
        o, w = _V[key]
        vec[:, o:o + w] = arr.reshape(128, w)

    for l in range(DEPTH):
        put(("bada", l), b_ada[l].reshape(48, 128).T)
        put(("qg", l), np.tile(f(inp["q_norm"])[l], 2)[:, None])
        put(("kg", l), np.tile(f(inp["k_norm"])[l], 2)[:, None])
        put(("hgn", l), np.tile(f(inp["hg_norm"])[l], 2)[:, None])
        for nme, key in (("ln1g", "ln1_g"), ("ln1b", "ln1_b"), ("ln2g", "ln2_g"), ("ln2b", "ln2_b")):
            put((nme, l), f(inp[key])[l].reshape(8, 128).T)
        put(("convw", l), f(inp["conv_w"])[l].reshape(3, 44, 128).transpose(2, 1, 0))
        put(("convb", l), f(inp["conv_b"])[l].reshape(44, 128).T)
        put(("ftw", l), f(inp["ft_w"])[l].reshape(2, 128, 64).transpose(1, 0, 2))
    put("hglb", f(inp["hg_lb"]).reshape(DEPTH, 2, 2, 128).transpose(3, 0, 1, 2))
    cst, msk, rope, dftl, dftc = host_consts()
    sh["cst"] = cst
    sh["msk"] = msk
    sh["rope"] = rope
    sh["dftl"] = dftl.reshape(8 * 2 * 128, 16 * 256)
    sh["dftc"] = dftc.reshape(2 * 128, 2 * 256)
    maps = []
    for b in range(B):
        m = dict(sh)
        xa = np.concatenate([ctx[b], x[b]], axis=0)
        m["xT0"] = np.ascontiguousarray(xa.T.reshape(8, 128, NT).transpose(1, 0, 2)).reshape(128, 8 * NT)
        v = vec.copy()
        o, w = _V["cc"]
        cc = np.stack([c[b], c_ctx], axis=1)
        v[:, o:o + w] = cc.reshape(8, 128, 2).transpose(1, 0, 2).reshape(128, 16)
        m["vecs"] = v
        maps.append(m)
    return maps


ARENA_WORDS = 53000


def build_program(n_layers=DEPTH, dbg=False):
    import contextlib
    nc = bass.Bass("TRN2", target_bir_lowering=False)
    dram = lambda n, s, d=F32, k="ExternalInput": nc.dram_tensor(n, list(s), d, kind=k).ap()
    xT0 = dram("xT0", [128, 8 * NT])
    vecs_d = dram("vecs", [128, NV])
    cst_d = dram("cst", [128, NC])
    msk_d = dram("msk", [128, 256], I32)
    rope_d = dram("rope", [128, 2 * T], BF16)
    dftl_d = dram("dftl", [8 * 2 * 128, 16 * 256], BF16)
    dftc_d = dram("dftc", [2 * 128, 2 * 256], BF16)
    wada_d = dram("wada", [DEPTH * 48 * 128, 1024])
    win_d = dram("win", [DEPTH * NSLAB * 128, 1024])
    wout_d = dram("wout", [DEPTH * 128, 8192])
    wup_d = dram("wup", [DEPTH * NJ * 128, 2048])
    wdown_d = dram("wdown", [DEPTH * 128, NJ * 1024])
    yT = dram("yT", [128, 8 * T], F32, "ExternalOutput")
    winb = dram("winb", [DEPTH * NSLAB * 128, 1024], BF16, "Internal")
    woutb = dram("woutb", [DEPTH * 128, 8192], BF16, "Internal")
    wupb = dram("wupb", [DEPTH * NJ * 128, 2048], BF16, "Internal")
    wdownb = dram("wdownb", [DEPTH * 128, NJ * 1024], BF16, "Internal")
    dbg_t = {}
    if dbg:
        for l in range(n_layers):
            dbg_t["mix%d" % l] = dram("d_mix%d" % l, [128, 8 * NT], BF16, "ExternalOutput")
            dbg_t["x1_%d" % l] = dram("d_x1_%d" % l, [128, 8 * NT], F32, "ExternalOutput")
            dbg_t["x_%d" % l] = dram("d_x_%d" % l, [128, 8 * NT], F32, "ExternalOutput")
        dbg_t["mod"] = dram("d_mod", [128, DEPTH * 96], F32, "ExternalOutput")

    st = contextlib.ExitStack()
    with st:
        art = st.enter_context(nc.sbuf_tensor("arena", [128, ARENA_WORDS], F32))
        pst = st.enter_context(nc.psum_tensor("ps", [128, 4096], F32))
        A = Arena(art, ARENA_WORDS)
        P = Prog(nc)

        def bank(b, n=512, off=0):
            return pst[:, b * 512 + off:b * 512 + off + n]

        def mm(out, lhsT, rhs, start=True, stop=True):
            P.op("pe", lambda e: e.matmul(out, lhsT=lhsT, rhs=rhs, start=start, stop=stop), [lhsT, rhs], [out])

        def tr(out, in_, ident):
            P.op("pe", lambda e: e.transpose(out, in_, ident), [in_, ident], [out])

        def act(out, in_, func, scale=1.0, bias=0.0):
            rd = [in_] + [a for a in (scale, bias) if not isinstance(a, (int, float))]
            P.op("act", lambda e: e.activation(out, in_, func, bias=bias, scale=scale), rd, [out])

        def tt(out, in0, in1, op, eng="dve"):
            P.op(eng, lambda e: e.tensor_tensor(out=out, in0=in0, in1=in1, op=op), [in0, in1], [out])

        def ts(out, in0, s1, s2, op0, op1=None, eng="dve"):
            rd = [in0] + [a for a in (s1, s2) if a is not None and not isinstance(a, (int, float))]
            if op1 is None:
                P.op(eng, lambda e: e.tensor_scalar(out=out, in0=in0, scalar1=s1, scalar2=None, op0=op0), rd, [out])
            else:
                P.op(eng, lambda e: e.tensor_scalar(out=out, in0=in0, scalar1=s1, scalar2=s2, op0=op0, op1=op1), rd, [out])

        def stt(out, in0, scalar, in1, op0, op1):
            rd = [in0, in1] + ([] if isinstance(scalar, (int, float)) else [scalar])
            P.op("dve", lambda e: e.scalar_tensor_tensor(out=out, in0=in0, scalar=scalar, in1=in1, op0=op0, op1=op1), rd, [out])

        def cp(out, in_, eng="dve"):
            if eng == "act":
                P.op("act", lambda e: e.copy(out, in_), [in_], [out])
            else:
                P.op(eng, lambda e: e.tensor_copy(out, in_), [in_], [out])

        def memset(ap, val, eng="pool"):
            P.op(eng, lambda e: e.memset(ap, val), [], [ap])

        def dma(out, in_, eng="sp"):
            P.op(eng, lambda e: e.dma_start(out=out, in_=in_), [in_], [out], dma=True)

        cst = A.f32(NC)
        C = lambda n: cst[:, _C[n][0]:_C[n][0] + _C[n][1]]
        msk = A.f32(256).bitcast(I32)
        vecs = A.f32(NV)

        def V(key, lo=0, n=None):
            o, w = _V[key]
            n = w - lo if n is None else n
            return vecs[:, o + lo:o + lo + n]

        rope = A.bf16(2 * T)
        ropecos = rope[:, 0:T]
        ropesin = rope[:, T:2 * T]
        identb = A.bf16(128)
        mod = A.f32(DEPTH * 96)
        mod1p = A.f32(DEPTH * 96)
        lbt = A.f32(DEPTH * 4 * 3)
        csil = A.f32(16)
        xT = A.f32(8 * NT)
        xv = xT.rearrange("p (c t) -> p c t", t=NT)
        mix_rf = A.bf16(4 * NT)
        mix_rf_v = mix_rf.rearrange("p (c t) -> p c t", t=NT)
        cst_f = [A.f32(512) for _ in range(2)]
        cst_b = [A.bf16(512) for _ in range(2)]
        slab = [A.bf16(1024) for _ in range(3)]
        slab_i = [0]
        Rj = [A.bf16(256) for _ in range(2)]

        def MOD(l, g, dc, j):
            o = l * 96 + (g * 8 + dc) * 2 + j
            return mod[:, o:o + 1]

        def MOD1P(l, g, dc, j):
            o = l * 96 + (g * 8 + dc) * 2 + j
            return mod1p[:, o:o + 1]

        def LB(l, d, fc, w):
            o = ((l * 2 + d) * 2 + fc) * 3 + w
            return lbt[:, o:o + 1]

        dma(cst, cst_d)
        dma(msk, msk_d)
        dma(vecs, vecs_d)
        dma(rope, rope_d)
        for dc in range(8):
            dma(xv[:, dc, :], xT0[:, dc * NT:(dc + 1) * NT])
        cp(identb, C("ident"), "dve")

        cast_list = []
        for l in range(n_layers):
            for s in range(NSLAB):
                r0 = (l * NSLAB + s) * 128
                for h in range(2):
                    cast_list.append((win_d[r0:r0 + 128, h * 512:(h + 1) * 512], winb[r0:r0 + 128, h * 512:(h + 1) * 512]))
            for h in range(16):
                cast_list.append((wout_d[l * 128:(l + 1) * 128, h * 512:(h + 1) * 512], woutb[l * 128:(l + 1) * 128, h * 512:(h + 1) * 512]))
            for j in range(NJ):
                r0 = (l * NJ + j) * 128
                for h in range(4):
                    cast_list.append((wup_d[r0:r0 + 128, h * 512:(h + 1) * 512], wupb[r0:r0 + 128, h * 512:(h + 1) * 512]))
            for h in range(NJ * 2):
                cast_list.append((wdown_d[l * 128:(l + 1) * 128, h * 512:(h + 1) * 512], wdownb[l * 128:(l + 1) * 128, h * 512:(h + 1) * 512]))
        import os
        ncast = min(len(cast_list), int(os.environ.get('PRO_CAST', '100000')))

        def cast_load(i):
            dma(cst_f[i % 2], cast_list[i][0], eng="pool")

        def cast_do(i):
            cp(cst_b[i % 2], cst_f[i % 2], "pool")
            dma(cast_list[i][1], cst_b[i % 2], eng="pool")

        if ncast > 0:
            cast_load(0)
        for i in range(ncast):
            if i + 1 < ncast:
                cast_load(i + 1)
            cast_do(i)

        act(csil, V("cc"), AF.Silu)
        csv = csil.rearrange("p (k j) -> p k j", j=2)

        def adaln(l):
            A.push()
            wst = [A.f32(1024) for _ in range(3)]
            for m in range(48):
                w = wst[m % 3]
                r0 = (l * 48 + m) * 128
                dma(w, wada_d[r0:r0 + 128, :])
                wv = w.rearrange("p (k c) -> p k c", c=128)
                for k in range(8):
                    mm(bank(7, 2, m * 2), wv[:, k, :], csv[:, k, :], start=(k == 0), stop=(k == 7))
            mv = mod[:, l * 96:(l + 1) * 96].rearrange("p (m j) -> p m j", j=2)
            if os.environ.get('PRO_ADA', '1') == '2':
                cp(mod[:, l * 96:(l + 1) * 96], bank(7, 96), "dve")
            else:
                pv_ = bank(7, 96).rearrange("p (m j) -> p m j", j=2)
                for j_ in range(2):
                    tt(mv[:, :, j_], pv_[:, :, j_], V(("bada", l)), ALU.add)
            ts(mod1p[:, l * 96:(l + 1) * 96], mod[:, l * 96:(l + 1) * 96], 1.0, None, ALU.add)
            A.pop()

        adaln(0)
        if dbg:
            dma(dbg_t["mod"], mod)
        hl = V("hglb")
        lbv = lbt.rearrange("p (l d f w) -> p l d f w", l=DEPTH, d=2, f=2)
        memset(lbt, 0.0, "dve")
        if n_layers > 1:
            A.push()
            tmp4 = A.f32(4)
            tt(tmp4, hl[:, 4:8], hl[:, 0:4], ALU.subtract)
            act(tmp4, tmp4, AF.Sigmoid)
            cp(lbv[:, 1, :, :, 0], tmp4.rearrange("p (d f) -> p d f", d=2), "dve")
            A.pop()
        for l in range(n_layers):
            ts(lbv[:, l, :, :, 1], lbv[:, l, :, :, 0], -1.0, 1.0, ALU.mult, ALU.add)
            ts(lbv[:, l, :, :, 2], lbv[:, l, :, :, 0], 1.0, -1.0, ALU.mult, ALU.add)

        def load_slab(l, name):
            s = slab[slab_i[0] % 3]
            slab_i[0] += 1
            r0 = (l * NSLAB + SLAB_IDX[name]) * 128
            dma(s, winb[r0:r0 + 128, :])
            return s.rearrange("p (k c) -> p k c", c=128)

        def modulate(dst, l, t0, n, j, gsc, gsh, dst_off=0):
            for dc in range(8):
                o = dst[:, dc, dst_off:dst_off + n]
                if dc % 2 == 0:
                    act(o, xv[:, dc, t0:t0 + n], AF.Identity, scale=MOD1P(l, gsc, dc, j), bias=MOD(l, gsh, dc, j))
                else:
                    ts(o, xv[:, dc, t0:t0 + n], MOD1P(l, gsc, dc, j), MOD(l, gsh, dc, j), ALU.mult, ALU.add)

        def proj_fm(ps, wsl, hT, n):
            for k in range(8):
                mm(ps, wsl[:, k, :], hT[:, k, 0:n], start=(k == 0), stop=(k == 7))

        def layer_norm(l, t0, n, gkey, bkey, scratch):
            sq, m_sb, r_sb, tb = scratch
            pm = bank(6, n)
            pq = bank(7, n)
            for dc in range(8):
                mm(pm, C("ones1024"), xv[:, dc, t0:t0 + n], start=(dc == 0), stop=(dc == 7))
            for dc in range(8):
                s = sq[dc % 2][:, 0:n]
                act(s, xv[:, dc, t0:t0 + n], AF.Square)
                mm(pq, C("ones1024"), s, start=(dc == 0), stop=(dc == 7))
            cp(m_sb[:, 0:n], pm, "act")
            tt(r_sb[:, 0:n], m_sb[:, 0:n], m_sb[:, 0:n], ALU.mult)
            tt(r_sb[:, 0:n], pq, r_sb[:, 0:n], ALU.subtract)
            act(r_sb[:, 0:n], r_sb[:, 0:n], AF.Ln, bias=EPS)
            act(r_sb[:, 0:n], r_sb[:, 0:n], AF.Exp, scale=-0.5)
            for dc in range(8):
                t = tb[dc % 2][:, 0:n]
                tt(t, xv[:, dc, t0:t0 + n], m_sb[:, 0:n], ALU.subtract)
                tt(t, t, r_sb[:, 0:n], ALU.mult)
                act(xv[:, dc, t0:t0 + n], t, AF.Identity, scale=V((gkey, l), dc, 1), bias=V((bkey, l), dc, 1))

        def scan_op(out, d0, d1_):
            P.op("dve", lambda e: e.tensor_tensor_scan(out=out, data0=d0, data1=d1_, initial=0.0,
                                                       op0=ALU.mult, op1=ALU.add), [d0, d1_], [out])

        def pred_copy(out, mask, data):
            P.op("dve", lambda e: e.copy_predicated(out=out, mask=mask, data=data), [mask, data, out], [out])

        def hgrn_phase(l, need_ctx):
            A.push()
            o_f = A.f32(2 * NT)
            o_fv = o_f.rearrange("p (c t) -> p c t", t=NT)
            gate = A.bf16(2 * NT)
            gatev = gate.rearrange("p (c t) -> p c t", t=NT)
            v_tm = A.bf16(18 * 256)
            v_tmv = v_tm.rearrange("p (i c) -> p i c", c=256)
            hTc = A.bf16(8 * 512).rearrange("p (k t) -> p k t", t=512)
            qf = [A.f32(512) for _ in range(2)]
            rr = [A.f32(512) for _ in range(2)]
            kkf = [A.f32(512) for _ in range(2)]
            cum = [A.f32(512) for _ in range(2)]
            d1 = [A.f32(512) for _ in range(2)]
            d4 = rr
            e3 = [A.f32(512) for _ in range(2)]
            Qt = [A.bf16(512) for _ in range(2)]
            Kt = [[A.bf16(512) for _ in range(2)] for _ in range(2)]
            Qh = [A.bf16(512) for _ in range(2)]
            Kh = [A.bf16(512) for _ in range(2)]
            osum = d1
            scT = [[A.bf16(128) for _ in range(2)] for _ in range(2)]
            Khtm = [[A.bf16(128) for _ in range(2)] for _ in range(2)]
            S = [A.f32(64) for _ in range(2)]
            Sbf = [[[A.bf16(64) for _ in range(2)] for _ in range(2)] for _ in range(2)]
            for a_ in scT + Kt + Khtm + [x_ for y_ in Sbf for x_ in y_]:
                for b_ in a_:
                    memset(b_, 0.0, "dve")
            for dirn in ((0, 1) if os.environ.get('HG_DIRS', '2') == '2' else (0,)):
                dn = "hff" if dirn == 0 else "hfb"
                mid, last = (31, 63) if dirn == 0 else (32, 0)
                for fc in range(2):
                    memset(S[fc], 0.0, "dve")
                    for hh in range(2):
                        memset(scT[fc][hh], 0.0, "dve")
                chunks = CH_ALL if dirn == 0 else [CH_ALL[0]] + CH_LAT[::-1]
                chunks = chunks[:int(os.environ.get('HG_CH', '9'))]
                TS = int(os.environ.get('HG_TSTEP', '9'))
                for (t0, n, j) in chunks:
                    is_ctx = (j == 1)
                    do_out = need_ctx or not is_ctx
                    ntile = n // 128
                    nch = n // 64
                    modulate(hTc, l, t0, n, j, 1, 0)
                    if dirn == 0:
                        for ti in range(ntile):
                            pv = bank(7, 256, 256)
                            for hh in range(2):
                                w = load_slab(l, "hi%d" % hh)
                                for k in range(8):
                                    mm(pv[:, hh * 128:(hh + 1) * 128], hTc[:, k, ti * 128:(ti + 1) * 128], w[:, k, :],
                                       start=(k == 0), stop=(k == 7))
                            cp(v_tmv[:, t0 // 128 + ti, :], pv, "act")
                    for fc in range(2):
                        proj_fm(bank(fc, n), load_slab(l, "hq%d" % fc), hTc, n)
                        act(qf[fc][:, 0:n], bank(fc, n), AF.Silu)
                    if dirn == 0 and do_out:
                        for fc in range(2):
                            proj_fm(bank(fc, n), load_slab(l, "hg%d" % fc), hTc, n)
                            act(gatev[:, fc, t0:t0 + n], bank(fc, n), AF.Silu)
                    for fc in range(2):
                        proj_fm(bank(2 + fc, n), load_slab(l, "%s%d" % (dn, fc)), hTc, n)
                    for fc in range(2):
                        act(rr[fc][:, 0:n], bank(2 + fc, n), AF.Sigmoid)
                    for fc in range(2):
                        ts(kkf[fc][:, 0:n], rr[fc][:, 0:n], LB(l, dirn, fc, 2), LB(l, dirn, fc, 1), ALU.mult, ALU.add)
                        ts(rr[fc][:, 0:n], rr[fc][:, 0:n], LB(l, dirn, fc, 1), LB(l, dirn, fc, 0), ALU.mult, ALU.add)
                    for fc in range(2):
                        act(rr[fc][:, 0:n], rr[fc][:, 0:n], AF.Ln)
                    sm = C("scanmask")[:, 0:n]
                    for fc in range(2):
                        if dirn == 0:
                            scan_op(cum[fc][:, 0:n], sm, rr[fc][:, 0:n])
                        else:
                            scan_op(cum[fc][:, 0:n][:, ::-1], sm, rr[fc][:, 0:n][:, ::-1])
                        c3 = cum[fc][:, 0:n].rearrange("p (c t) -> p c t", t=64)
                        tt(d1[fc][:, 0:n].rearrange("p (c t) -> p c t", t=64), c3,
                           c3[:, :, mid:mid + 1].to_broadcast([128, nch, 64]), ALU.subtract)
                        tt(d4[fc][:, 0:n].rearrange("p (c t) -> p c t", t=64),
                           c3[:, :, last:last + 1].to_broadcast([128, nch, 64]), c3, ALU.subtract)
                    for fc in range(2):
                        act(e3[fc][:, 0:n], cum[fc][:, 0:n], AF.Exp)
                        act(d4[fc][:, 0:n], d4[fc][:, 0:n], AF.Exp)
                        if do_out:
                            act(cum[fc][:, 0:n], d1[fc][:, 0:n], AF.Exp)
                            act(d1[fc][:, 0:n], d1[fc][:, 0:n], AF.Exp, scale=-1.0)
                    for fc in range(2):
                        tt(Qh[fc][:, 0:n], qf[fc][:, 0:n], e3[fc][:, 0:n], ALU.mult)
                        tt(Kh[fc][:, 0:n], kkf[fc][:, 0:n], d4[fc][:, 0:n], ALU.mult, eng="dve")
                        if do_out:
                            tt(Qt[fc][:, 0:n], qf[fc][:, 0:n], cum[fc][:, 0:n], ALU.mult)
                            for hh in range(2):
                                hs = slice(hh * 64, (hh + 1) * 64)
                                tt(Kt[fc][hh][hs, 0:n], kkf[fc][hs, 0:n], d1[fc][hs, 0:n], ALU.mult, eng="dve")
                    tiles = list(range(ntile)) if dirn == 0 else list(range(ntile))[::-1]
                    order = (0, 1) if dirn == 0 else (1, 0)
                    for ti in (tiles if os.environ.get('HG_TILES', '1') == '1' else []):
                        a = ti * 128
                        gi = t0 // 128 + ti
                        for fc in range(2):
                            pt = bank(4 if fc == 0 else 0, 128, 0)
                            mm(pt, Kh[fc][:, a:a + 128], identb)
                            for cc in range(2):
                                act(Khtm[fc][cc], pt, AF.Identity, scale=C("hmask")[:, cc:cc + 1])
                            po = bank(6 if fc == 0 else 2, 128, 0)
                            if TS < 2:
                                continue
                            if do_out:
                                for hh in range(2):
                                    hs = slice(hh * 64, (hh + 1) * 64)
                                    psc = bank(5 if fc == 0 else 1, 128, hh * 128)
                                    mm(psc, Kt[fc][hh][:, a:a + 128], Qt[fc][:, a:a + 128])
                                    pred_copy(scT[fc][hh], msk[:, dirn * 128:(dirn + 1) * 128], psc)
                            if TS < 3:
                                continue
                            for ci, cc in enumerate(order):
                                if do_out:
                                    for hh in range(2):
                                        hs = slice(hh * 64, (hh + 1) * 64)
                                        cp(Sbf[fc][ci][hh][hs, :], S[fc][hs, :], "act")
                                pu = bank(7 if fc == 0 else 3, 64, ci * 64)
                                for hh in range(2):
                                    hs = slice(hh * 64, (hh + 1) * 64)
                                    vcol = (fc * 2 + hh) * 64
                                    mm(pu[hs, :], Khtm[fc][cc][:, hs], v_tmv[:, gi, vcol:vcol + 64])
                                dcol = a + cc * 64 + last
                                stt(S[fc], S[fc], e3[fc][:, dcol:dcol + 1], pu, ALU.mult, ALU.add)
                            if TS < 4:
                                continue
                            if do_out:
                                for hh in range(2):
                                    hs = slice(hh * 64, (hh + 1) * 64)
                                    vcol = (fc * 2 + hh) * 64
                                    mm(po[hs, :], v_tmv[:, gi, vcol:vcol + 64], scT[fc][hh], start=True, stop=False)
                                    for ci, cc in enumerate(order):
                                        mm(po[hs, cc * 64:(cc + 1) * 64], Sbf[fc][ci][hh],
                                           Qh[fc][:, a + cc * 64:a + (cc + 1) * 64], start=False, stop=(ci == 1))
                                if dirn == 0:
                                    cp(o_fv[:, fc, t0 + a:t0 + a + 128], po, "act")
                                else:
                                    tt(osum[fc][:, a:a + 128], po, o_fv[:, fc, t0 + a:t0 + a + 128], ALU.add)
                    if dirn == 1 and do_out:
                        for fc in range(2):
                            act(qf[fc][:, 0:n], osum[fc][:, 0:n], AF.Square)
                            mm(bank(fc, n), C("blk64"), qf[fc][:, 0:n])
                        for fc in range(2):
                            act(qf[fc][:, 0:n], bank(fc, n), AF.Ln, bias=EPS)
                        for fc in range(2):
                            act(qf[fc][:, 0:n], qf[fc][:, 0:n], AF.Exp, scale=-0.5)
                            stt(osum[fc][:, 0:n], osum[fc][:, 0:n], V(("hgn", l)), qf[fc][:, 0:n], ALU.mult, ALU.mult)
                            tt(mix_rf_v[:, fc, t0:t0 + n], osum[fc][:, 0:n], gatev[:, fc, t0:t0 + n], ALU.mult)
            A.pop()

        def fourier_phase(l, need_ctx):
            A.push()
            hTc = A.bf16(8 * 512).rearrange("p (k t) -> p k t", t=512)
            uTb = [A.bf16(512) for _ in range(2)]
            PQ = A.bf16(18 * 2 * 256).rearrange("p (i j c) -> p i j c", j=2, c=256)
            tabs = [[A.bf16(16 * 256).rearrange("p (i c) -> p i c", c=256) for _ in range(2)] for _ in range(2)]
            ftw = V(("ftw", l)).rearrange("p (j d) -> p j d", d=64)
            for j in range(2):
                memset(Rj[j], 0.0, "dve")
                pa = bank(0, 64)
                pb = bank(1, 64)
                mm(pa, C("c64"), ftw[:, j, :])
                mm(pb, C("s64"), ftw[:, j, :])
                for g in range(2):
                    gs = slice(g * 64, (g + 1) * 64)
                    cp(Rj[j][gs, g * 64:(g + 1) * 64], pa[gs, :], "dve")
                    cp(Rj[j][gs, 128 + g * 64:128 + (g + 1) * 64], pb[gs, :], "dve")
            chunks = CH_ALL if need_ctx else CH_LAT
            for (t0, n, jm) in chunks:
                modulate(hTc, l, t0, n, jm, 1, 0)
                for j in range(2):
                    proj_fm(bank(j, n), load_slab(l, "ft%d" % j), hTc, n)
                    cp(uTb[j][:, 0:n], bank(j, n), "act")
                for ti in range(n // 128):
                    for j in range(2):
                        pp = bank(2 + j, 256)
                        mm(pp, uTb[j][:, ti * 128:(ti + 1) * 128], Rj[j])
                        cp(PQ[:, t0 // 128 + ti, j, :], pp, "dve" if j == 0 else "act")
            for nn in range(8):
                tb = tabs[nn % 2]
                for cs in range(2):
                    r0 = (nn * 2 + cs) * 128
                    dma(tb[cs], dftl_d[r0:r0 + 128, :].rearrange("p (i c) -> p i c", c=256))
                for j in range(2):
                    pf = bank(4 + j, 256)
                    for ti in range(16):
                        for cs in range(2):
                            mm(pf, PQ[:, 2 + ti, j, cs * 128:(cs + 1) * 128], tb[cs][:, ti, :],
                               start=(ti == 0 and cs == 0), stop=(ti == 15 and cs == 1))
                    cp(mix_rf_v[:, 2 + j, CT + nn * 256:CT + (nn + 1) * 256], pf, "act" if j == 0 else "dve")
            if need_ctx:
                tb = tabs[0]
                for cs in range(2):
                    dma(tb[cs][:, 0:2, :], dftc_d[cs * 128:(cs + 1) * 128, :].rearrange("p (i c) -> p i c", c=256))
                for j in range(2):
                    pf = bank(4 + j, 256)
                    for ti in range(2):
                        for cs in range(2):
                            mm(pf, PQ[:, ti, j, cs * 128:(cs + 1) * 128], tb[cs][:, ti, :],
                               start=(ti == 0 and cs == 0), stop=(ti == 1 and cs == 1))
                    cp(mix_rf_v[:, 2 + j, 0:CT], pf, "act" if j == 0 else "dve")
            if l == 0 and n_layers > 1:
                adaln(1)
            A.pop()

        def attention_phase(l, need_ctx, mix_a):
            A.push()
            qT = A.bf16(4 * NT).rearrange("p (c t) -> p c t", t=NT)
            kTp = [[A.bf16(NT) for _ in range(2)] for _ in range(2)]
            Vaug = A.bf16(18 * 2 * 192).rearrange("p (i h c) -> p i h c", h=2, c=192)
            A.push()
            hTc = A.bf16(8 * 512).rearrange("p (k t) -> p k t", t=512)
            sq = A.f32(512)
            rs = A.f32(512)
            qn = A.f32(512)
            t1 = sq
            A.pop()
            PT = [A.bf16(1024) for _ in range(2)]
            o_sb = [A.f32(512) for _ in range(2)]
            r_row = {0: A.f32(512), 64: A.f32(512)}
            for a_ in kTp:
                for b_ in a_:
                    memset(b_, 0.0, "dve")
            memset(Vaug, 1.0, "dve")

            def qk_chunk(name, gkey, dst, t0, n, rope_on):
                ps = bank(0, n)
                proj_fm(ps, load_slab(l, name), hTc, n)
                act(sq[:, 0:n], ps, AF.Square)
                pm = bank(1, n)
                mm(pm, C("blk64"), sq[:, 0:n])
                act(rs[:, 0:n], pm, AF.Ln, bias=EPS)
                act(rs[:, 0:n], rs[:, 0:n], AF.Exp, scale=-0.5)
                if not rope_on:
                    for (d_, hs) in dst:
                        stt(d_[hs, :], ps[hs, :], V((gkey, l))[hs, :], rs[hs, 0:n], ALU.mult, ALU.mult)
                    return
                stt(qn[:, 0:n], ps, V((gkey, l)), rs[:, 0:n], ALU.mult, ALU.mult)
                pr = bank(2, n)
                mm(pr, C("rperm"), qn[:, 0:n])
                lt = t0 - CT
                tt(t1[:, 0:n], qn[:, 0:n], ropecos[:, lt:lt + n], ALU.mult)
                tt(qn[:, 0:n], pr, ropesin[:, lt:lt + n], ALU.mult)
                for (d_, hs) in dst:
                    tt(d_[hs, :], t1[hs, 0:n], qn[hs, 0:n], ALU.add)

            for (t0, n, jm) in CH_ALL:
                is_ctx = (jm == 1)
                modulate(hTc, l, t0, n, jm, 1, 0)
                w = load_slab(l, "v")
                for ti in range(n // 128):
                    pv = bank(3, 128)
                    for k in range(8):
                        mm(pv, hTc[:, k, ti * 128:(ti + 1) * 128], w[:, k, :], start=(k == 0), stop=(k == 7))
                    cp(Vaug[:, t0 // 128 + ti, :, 64:128], pv.rearrange("p (h c) -> p h c", c=64), "act")
                lo_, hi_ = slice(0, 64), slice(64, 128)
                qk_chunk("ka", "kg", [(kTp[0][0][:, t0:t0 + n], lo_), (kTp[1][1][:, t0:t0 + n], hi_)], t0, n, not is_ctx)
                qk_chunk("kb", "kg", [(kTp[1][0][:, t0:t0 + n], lo_), (kTp[0][1][:, t0:t0 + n], hi_)], t0, n, not is_ctx)
                if (not is_ctx) or need_ctx:
                    for qc in range(4):
                        qk_chunk("q%d" % qc, "qg", [(qT[:, qc, t0:t0 + n], slice(0, 128))], t0, n, not is_ctx)

            def attend(q0, nq, ktiles):
                nsub = (nq + 511) // 512
                for h in range(8):
                    qc, hh, kvh = h // 2, h % 2, h // 4
                    hs = slice(hh * 64, (hh + 1) * 64)
                    def scores(ki):
                        kt = ktiles[ki]
                        pst_ = pst[:, (ki % 2) * 1024:(ki % 2) * 1024 + nq]
                        for s_ in range(nsub):
                            w_ = min(512, nq - s_ * 512)
                            mm(pst_[:, s_ * 512:s_ * 512 + w_], kTp[kvh][hh][:, kt * 128:(kt + 1) * 128],
                               qT[:, qc, q0 + s_ * 512:q0 + s_ * 512 + w_])

                    scores(0)
                    for ki, kt in enumerate(ktiles):
                        if ki + 1 < len(ktiles):
                            scores(ki + 1)
                        pst_ = pst[:, (ki % 2) * 1024:(ki % 2) * 1024 + nq]
                        p_ = PT[ki % 2][:, 0:nq]
                        act(p_, pst_, AF.Exp, scale=0.125)
                        vl = Vaug[:, kt, kvh, 64:192] if hh == 0 else Vaug[:, kt, kvh, 0:128]
                        for s_ in range(nsub):
                            w_ = min(512, nq - s_ * 512)
                            mm(bank(4 + s_, w_), vl, p_[:, s_ * 512:s_ * 512 + w_],
                               start=(ki == 0), stop=(ki == len(ktiles) - 1))
                    for s_ in range(nsub):
                        w_ = min(512, nq - s_ * 512)
                        ob = o_sb[s_][:, 0:w_]
                        cp(ob, bank(4 + s_, w_), "act")
                        drow = 64 if hh == 0 else 0
                        rr_ = r_row[drow]
                        P.op("dve", (lambda o_, i_: lambda e: e.reciprocal(o_, i_))(
                            rr_[drow:drow + 1, 0:w_], ob[drow:drow + 1, :]), [ob[drow:drow + 1, :]], [rr_[drow:drow + 1, 0:w_]])
                        pb = bank(6, w_)
                        mm(pb, C("ones"), rr_[:, 0:w_])
                        tt(mix_a[hs, qc, q0 + s_ * 512:q0 + s_ * 512 + w_], ob[hs, :], pb[hs, :], ALU.mult)

            memset(r_row[0], 0.0, "dve")
            memset(r_row[64], 0.0, "dve")
            for half in range(2):
                attend(CT + half * 1024, 1024, list(range(18)))
            if need_ctx:
                attend(0, 256, [0, 1])
            A.pop()

        def outproj_phase(l, need_ctx, mix_a):
            A.push()
            sq = [A.f32(512) for _ in range(2)]
            m_sb = A.f32(512)
            r_sb = A.f32(512)
            tb = [A.f32(512) for _ in range(2)]
            wo = A.bf16(8192).rearrange("p (k c) -> p k c", c=1024)
            dma(wo, woutb[l * 128:(l + 1) * 128, :].rearrange("p (k c) -> p k c", c=1024))
            chunks = CH_ALL if need_ctx else CH_LAT
            for (t0, n, jm) in chunks:
                for dc in range(8):
                    ps = bank(dc % 4, n)
                    for mc in range(8):
                        src = mix_a[:, mc, t0:t0 + n] if mc < 4 else mix_rf_v[:, mc - 4, t0:t0 + n]
                        mm(ps, wo[:, mc, dc * 128:(dc + 1) * 128], src, start=(mc == 0), stop=(mc == 7))
                    t = tb[dc % 2][:, 0:n]
                    act(t, ps, AF.Identity, scale=MOD(l, 2, dc, jm))
                    stt(xv[:, dc, t0:t0 + n], xv[:, dc, t0:t0 + n], ALPHA, t, ALU.mult, ALU.add)
                layer_norm(l, t0, n, "ln1g", "ln1b", (sq, m_sb, r_sb, tb))
            A.pop()

        def ffn_phase(l, need_ctx, last):
            A.push()
            sq = [A.f32(512) for _ in range(2)]
            m_sb = A.f32(512)
            r_sb = A.f32(512)
            tb = [A.f32(512) for _ in range(2)]
            wds = [A.bf16(NJ * 128).rearrange("p (j c) -> p j c", c=128) for _ in range(3)]
            NCM = 412
            h2 = A.bf16(8 * NCM).rearrange("p (k t) -> p k t", t=NCM)
            actb = A.bf16(NJ * 410).rearrange("p (j t) -> p j t", t=410)
            NWU = 5
            wu = [A.bf16(2048).rearrange("p (k c) -> p k c", c=256) for _ in range(NWU)]
            cA = [A.f32(410) for _ in range(2)]
            cG = [A.f32(410) for _ in range(2)]
            sG = [A.bf16(410) for _ in range(2)]
            hprev = A.bf16(8).rearrange("p (k t) -> p k t", t=1)
            cw = V(("convw", l)).rearrange("p (c t) -> p c t", t=3)
            cb = V(("convb", l))
            seqs = ([(0, CT, 1, FFN_CTX)] if need_ctx else []) + [(CT, T, 0, FFN_LAT)]
            ui = 0
            di = 0
            pending = []

            def flush_ln():
                while pending:
                    (t0_, n_, jm_) = pending.pop(0)
                    layer_norm(l, t0_, n_, "ln2g", "ln2b", (sq, m_sb, r_sb, tb))
                    if last and jm_ == 0:
                        for dc_ in range(8):
                            dma(yT[:, dc_ * T + (t0_ - CT):dc_ * T + (t0_ - CT) + n_], xv[:, dc_, t0_:t0_ + n_])
            for (s0, slen, jm, chl) in seqs:
                for (t0, n) in chl:
                    lo = t0 - 1
                    hi = t0 + n + 1
                    a1 = min(hi, s0 + slen)
                    if lo < s0:
                        memset(h2[:, :, 0:1], 0.0, "dve")
                    else:
                        cp(h2[:, :, 0:1], hprev, "dve")
                    if hi > s0 + slen:
                        memset(h2[:, :, n + 1:n + 2], 0.0, "dve")
                    modulate(h2, l, t0, a1 - t0, jm, 4, 3, dst_off=1)
                    cp(hprev, h2[:, :, n:n + 1], "dve")
                    for j in range(NJ):
                        w = wu[ui % NWU]
                        ui += 1
                        r0 = (l * NJ + j) * 128
                        dma(w, wupb[r0:r0 + 128, :].rearrange("p (k c) -> p k c", c=256))
                        pa = bank(j % 2, n + 2)
                        pg = bank(2 + j % 2, n + 2)
                        for k in range(8):
                            mm(pa, w[:, k, 0:128], h2[:, k, 0:n + 2], start=(k == 0), stop=(k == 7))
                        for k in range(8):
                            mm(pg, w[:, k, 128:256], h2[:, k, 0:n + 2], start=(k == 0), stop=(k == 7))
                        ca = cA[j % 2][:, 0:n]
                        cg = cG[j % 2][:, 0:n]
                        sg = sG[j % 2][:, 0:n]
                        for (pp, cc_, fi) in ((pa, ca, j), (pg, cg, NJ + j)):
                            act(cc_, pp[:, 1:n + 1], AF.Identity, scale=cw[:, fi, 1:2], bias=cb[:, fi:fi + 1])
                            stt(cc_, pp[:, 0:n], cw[:, fi, 0:1], cc_, ALU.mult, ALU.add)
                            stt(cc_, pp[:, 2:n + 2], cw[:, fi, 2:3], cc_, ALU.mult, ALU.add)
                        act(sg, cg, AF.Silu)
                        tt(actb[:, j, 0:n], ca, sg, ALU.mult)
                    flush_ln()
                    for dc in range(8):
                        ps = bank(4 + dc % 2, n)
                        wd = wds[di % 3]
                        di += 1
                        dma(wd, wdownb[l * 128:(l + 1) * 128, dc * NJ * 128:(dc + 1) * NJ * 128].rearrange("p (j c) -> p j c", c=128))
                        for j in range(NJ):
                            mm(ps, wd[:, j, :], actb[:, j, 0:n], start=(j == 0), stop=(j == NJ - 1))
                        t = tb[dc % 2][:, 0:n]
                        act(t, ps, AF.Identity, scale=MOD(l, 5, dc, jm))
                        stt(xv[:, dc, t0:t0 + n], xv[:, dc, t0:t0 + n], ALPHA, t, ALU.mult, ALU.add)
                    pending.append((t0, n, jm))
            flush_ln()
            A.pop()

        def dump_x(key):
            if dbg and key in dbg_t:
                dma(dbg_t[key], xT)

        stop = getattr(build_program, "stop", "")
        for l in range(n_layers):
            need_ctx = l < DEPTH - 1
            if stop == "prologue":
                break
            if "nohgrn" not in stop:
                hgrn_phase(l, need_ctx)
            if stop == "hgrn":
                dma(dbg_t["mix%d" % l][:, 4 * NT:8 * NT], mix_rf)
                break
            if "nofourier" not in stop:
                fourier_phase(l, need_ctx)
            if stop.startswith("fourier"):
                dma(dbg_t["mix%d" % l][:, 4 * NT:8 * NT], mix_rf)
                break
            A.push()
            mix_a = A.bf16(4 * NT).rearrange("p (c t) -> p c t", t=NT)
            attention_phase(l, need_ctx, mix_a)
            if stop.startswith("attn"):
                dma(dbg_t["mix%d" % l][:, 0:4 * NT].rearrange("p (c t) -> p c t", t=NT), mix_a)
                break
            if dbg:
                mk = dbg_t["mix%d" % l]
                dma(mk[:, 0:4 * NT].rearrange("p (c t) -> p c t", t=NT), mix_a)
                dma(mk[:, 4 * NT:8 * NT], mix_rf)
            outproj_phase(l, need_ctx, mix_a)
            A.pop()
            dump_x("x1_%d" % l)
            ffn_phase(l, need_ctx, last=(l == n_layers - 1))
            dump_x("x_%d" % l)
        P.emit()
        build_program.stats = P.stats
        build_program.arena_peak = A.peak
    return nc


_CACHE = {}


def kernel(**inputs):
    maps = host_layout(inputs)
    if "nc" not in _CACHE:
        _CACHE["nc"] = build_program()
    nc = _CACHE["nc"]
    res = run_bass_kernel_spmd(nc, maps, core_ids=list(range(len(maps))))
    outs = []
    for r in res.results:
        y = np.asarray(r["yT"]).reshape(128, 8, T)
        outs.append(np.ascontiguousarray(y.transpose(2, 1, 0)).reshape(T, D))
    return np.stack(outs, axis=0).astype(np.float32)
```
